# Optimizing a Trainium2 kernel written in Bass

```python
import math
import numpy as np
import jax
import jax.numpy as jnp
from jax import lax

D_MODEL = 2048
BATCH = 16
SEQ = 2048
DEPTH = 4

HEAD_DIM = 128
CONV_DIM = D_MODEL // 4
CONV_GROUPS = CONV_DIM // HEAD_DIM
GDN_HEADS = (D_MODEL - CONV_DIM) // (2 * HEAD_DIM)
NSA_HEADS = GDN_HEADS
GDN_DIM = GDN_HEADS * HEAD_DIM
NSA_DIM = NSA_HEADS * HEAD_DIM
NSA_KV_HEADS = 2
NSA_GROUP = NSA_HEADS // NSA_KV_HEADS
NSA_KV_DIM = 6 * NSA_KV_HEADS * HEAD_DIM
GDN_CONV_W = 4
GDN_CHUNK = 64
L_CMP = 32
D_CMP = 16
L_SLC = 64
TOP_N = 8
WINDOW = 512
Q_BLOCK = 128
SEL_Q_BLOCK = 64
NUM_BUCKETS = 32
MAX_DISTANCE = 128
SHORT_CONV_W = 3
D_FF = ((8 * D_MODEL // 3 + 255) // 256) * 256
PROJ_SIZES = (3 * GDN_DIM, GDN_DIM, GDN_HEADS, GDN_HEADS, NSA_DIM, NSA_KV_DIM, 3 * NSA_HEADS, CONV_DIM, CONV_DIM, CONV_DIM)
PROJ_DIM = sum(PROJ_SIZES)
RMS_EPS = 1e-6
FORCED_SCORE = 1e4

kernel_name = "hybrid_gdn_nsa_shortconv_trunk"


def rms_norm(x, g):
    xf = x.astype(jnp.float32)
    y = xf * lax.rsqrt(jnp.mean(xf * xf, axis=-1, keepdims=True) + RMS_EPS)
    return (y * g.astype(jnp.float32)).astype(x.dtype)


def l2_norm(x):
    return x * lax.rsqrt(jnp.sum(x * x, axis=-1, keepdims=True) + RMS_EPS)


def causal_depthwise_conv(x, w):
    S = x.shape[1]
    K = w.shape[-1]
    xp = jnp.pad(x, ((0, 0), (K - 1, 0), (0, 0)))
    y = xp[:, 0:S, :] * w[:, 0]
    for j in range(1, K):
        y = y + xp[:, j:j + S, :] * w[:, j]
    return y


def masked_softmax(logits, mask):
    logits = jnp.where(mask, logits, -jnp.inf)
    m = jnp.max(logits, axis=-1, keepdims=True)
    m = jnp.where(jnp.isfinite(m), m, 0.0)
    e = jnp.exp(logits - m)
    den = jnp.sum(e, axis=-1, keepdims=True)
    return e / jnp.where(den > 0, den, 1.0)


def t5_bucket(dist):
    n = jnp.maximum(dist, 0)
    max_exact = NUM_BUCKETS // 2
    nf = jnp.maximum(n, 1).astype(jnp.float32)
    large = max_exact + (jnp.log(nf / max_exact) / math.log(MAX_DISTANCE / max_exact) * (NUM_BUCKETS - max_exact)).astype(jnp.int32)
    large = jnp.minimum(large, NUM_BUCKETS - 1)
    return jnp.where(n < max_exact, n, large)


def head_bias(rel_bias, dist):
    b = rel_bias[t5_bucket(dist)].astype(jnp.float32)
    b = jnp.moveaxis(b, -1, 0)
    return b.reshape((NSA_KV_HEADS, NSA_GROUP) + dist.shape)


def gated_delta_rule_chunked(q, k, v, g, beta):
    B, H, S, dk = q.shape
    dv = v.shape[-1]
    C = GDN_CHUNK
    N = S // C
    q = q.reshape(B, H, N, C, dk)
    k = k.reshape(B, H, N, C, dk)
    v = v.reshape(B, H, N, C, dv)
    g = g.reshape(B, H, N, C)
    beta = beta.reshape(B, H, N, C)
    gc = jnp.cumsum(g, axis=-1)
    tril = jnp.asarray(np.tril(np.ones((C, C), dtype=bool)))
    strict = jnp.asarray(np.tril(np.ones((C, C), dtype=bool), -1))
    eye = jnp.eye(C, dtype=jnp.float32)
    diff = gc[..., :, None] - gc[..., None, :]
    decay = jnp.where(tril, jnp.exp(jnp.where(tril, diff, 0.0)), 0.0)
    kb = k * beta[..., None]
    vb = v * beta[..., None]
    L = jnp.where(strict, jnp.einsum('bhncd,bhnsd->bhncs', kb, k) * decay, 0.0)
    rhs = jnp.concatenate([vb, kb * jnp.exp(gc)[..., None]], axis=-1)
    sol = lax.linalg.triangular_solve(L + eye, rhs, left_side=True, lower=True, unit_diagonal=True)
    u = sol[..., :dv]
    w = sol[..., dv:]
    attn = jnp.where(tril, jnp.einsum('bhncd,bhnsd->bhncs', q, k) * decay, 0.0)

    def step(state, inp):
        q_i, k_i, u_i, w_i, gc_i, attn_i = inp
        v_new = u_i - jnp.einsum('bhcd,bhde->bhce', w_i, state)
        o = jnp.einsum('bhcd,bhde->bhce', q_i * jnp.exp(gc_i)[..., None], state) + jnp.einsum('bhcs,bhse->bhce', attn_i, v_new)
        g_last = gc_i[..., -1]
        k_dec = k_i * jnp.exp(g_last[..., None] - gc_i)[..., None]
        state = state * jnp.exp(g_last)[..., None, None] + jnp.einsum('bhcd,bhce->bhde', k_dec, v_new)
        return state, o

    xs = tuple(jnp.moveaxis(t, 2, 0) for t in (q, k, u, w, gc, attn))
    state0 = jnp.zeros((B, H, dk, dv), jnp.float32)
    _, o = lax.scan(step, state0, xs)
    return jnp.moveaxis(o, 0, 2).reshape(B, H, S, dv)


def gdn_mixer(qkv, z, b, a, conv_w, a_log, dt_bias, norm_g):
    B, S, _ = qkv.shape
    H, dk = GDN_HEADS, HEAD_DIM
    qkv = jax.nn.silu(causal_depthwise_conv(qkv, conv_w))
    q, k, v = [t.reshape(B, S, H, dk).transpose(0, 2, 1, 3).astype(jnp.float32) for t in jnp.split(qkv, 3, axis=-1)]
    q = l2_norm(q) * (dk ** -0.5)
    k = l2_norm(k)
    beta = jax.nn.sigmoid(b.astype(jnp.float32)).transpose(0, 2, 1)
    g = (-jnp.exp(a_log.astype(jnp.float32)) * jax.nn.softplus(a.astype(jnp.float32) + dt_bias.astype(jnp.float32))).transpose(0, 2, 1)
    o = gated_delta_rule_chunked(q, k, v, g, beta)
    o = o.transpose(0, 2, 1, 3).astype(z.dtype)
    o = rms_norm(o, norm_g) * jax.nn.silu(z.reshape(B, S, H, dk))
    return o.reshape(B, S, H * dk)


def nsa_mixer(q, kv, gates, q_norm, k_norm, cmp_pos, cmp_w1, cmp_w2, rel_bias):
    B, S, _ = q.shape
    H, Hkv, G, dk = NSA_HEADS, NSA_KV_HEADS, NSA_GROUP, HEAD_DIM
    scale = dk ** -0.5
    t = jnp.arange(S, dtype=jnp.int32)
    q = rms_norm(q.reshape(B, S, H, dk), q_norm)
    q = q.reshape(B, S, Hkv, G, dk).transpose(0, 2, 3, 1, 4)
    kv = kv.reshape(B, S, 6, Hkv, dk).transpose(2, 0, 3, 1, 4)
    k_cmp_tok, v_cmp_tok = kv[0], kv[1]
    k_slc, v_slc = rms_norm(kv[2], k_norm[1]), kv[3]
    k_win, v_win = rms_norm(kv[4], k_norm[2]), kv[5]

    n_cmp = (S - L_CMP) // D_CMP + 1
    cmp_start = np.arange(n_cmp) * D_CMP
    cmp_idx = cmp_start[:, None] + np.arange(L_CMP)[None, :]
    cmp_end = jnp.asarray(cmp_start + L_CMP - 1, dtype=jnp.int32)

    def compress(tok, pe, w1, w2):
        blk = tok[:, :, cmp_idx] + pe
        return jax.nn.silu(blk.reshape(B, Hkv, n_cmp, L_CMP * dk) @ w1) @ w2

    kc = rms_norm(compress(k_cmp_tok, cmp_pos[0], cmp_w1[0], cmp_w2[0]), k_norm[0])
    vc = compress(v_cmp_tok, cmp_pos[1], cmp_w1[1], cmp_w2[1])
    dist_c = t[:, None] - cmp_end[None, :]
    s_c = jnp.einsum('bhgtd,bhnd->bhgtn', q, kc).astype(jnp.float32) * scale + head_bias(rel_bias, dist_c)
    p_c = masked_softmax(s_c, dist_c >= 0)
    o_cmp = jnp.einsum('bhgtn,bhnd->bhgtd', p_c.astype(vc.dtype), vc)

    n_sel = S // L_SLC
    top_n = min(TOP_N, n_sel)
    sel_start = np.arange(n_sel) * L_SLC
    overlap = ((cmp_start[:, None] < sel_start[None, :] + L_SLC) & (cmp_start[:, None] + L_CMP > sel_start[None, :])).astype(np.float32)
    imp = jnp.einsum('bhgtn,nj->bhtj', p_c, jnp.asarray(overlap))
    cur = t // L_SLC
    j = jnp.arange(n_sel, dtype=jnp.int32)
    future = j[None, :] > cur[:, None]
    forced = (j[None, :] == 0) | (j[None, :] == cur[:, None]) | (j[None, :] == cur[:, None] - 1)
    imp = jnp.where(future, -1.0, jnp.where(forced, FORCED_SCORE, imp))
    _, sel_idx = lax.top_k(imp, top_n)
    sel_idx = sel_idx.astype(jnp.int32)

    nq = S // SEL_Q_BLOCK
    kb = k_slc.reshape(B * Hkv * n_sel, L_SLC, dk)
    vb = v_slc.reshape(B * Hkv * n_sel, L_SLC, dk)
    base = (jnp.arange(B * Hkv, dtype=jnp.int32) * n_sel).reshape(B, Hkv, 1, 1)
    flat = base + sel_idx
    q_ch = q.reshape(B, Hkv, G, nq, SEL_Q_BLOCK, dk).transpose(3, 0, 1, 2, 4, 5)
    flat_ch = flat.reshape(B, Hkv, nq, SEL_Q_BLOCK, top_n).transpose(2, 0, 1, 3, 4)
    sel_ch = sel_idx.reshape(B, Hkv, nq, SEL_Q_BLOCK, top_n).transpose(2, 0, 1, 3, 4)
    t_ch = t.reshape(nq, SEL_Q_BLOCK)
    tab = rel_bias.T.reshape(Hkv, G, NUM_BUCKETS)
    M = top_n * L_SLC

    def bias_lookup(tab_h, bk_h):
        return jnp.moveaxis(tab_h[:, bk_h], 0, 1)

    def sel_block(args):
        qc, fc, sc, tc = args
        kg = kb[fc].reshape(B, Hkv, SEL_Q_BLOCK, M, dk)
        vg = vb[fc].reshape(B, Hkv, SEL_Q_BLOCK, M, dk)
        pos = (sc[..., None] * L_SLC + jnp.arange(L_SLC, dtype=jnp.int32)).reshape(B, Hkv, SEL_Q_BLOCK, M)
        dist = tc[None, None, :, None] - pos
        bias = jax.vmap(bias_lookup, in_axes=(0, 1), out_axes=1)(tab, t5_bucket(dist))
        s = jnp.einsum('bhgqd,bhqmd->bhgqm', qc, kg).astype(jnp.float32) * scale + bias.astype(jnp.float32)
        p = masked_softmax(s, (dist >= 0)[:, :, None])
        return jnp.einsum('bhgqm,bhqmd->bhgqd', p.astype(vg.dtype), vg)

    o_slc = lax.map(sel_block, (q_ch, flat_ch, sel_ch, t_ch))
    o_slc = o_slc.transpose(1, 2, 3, 0, 4, 5).reshape(B, Hkv, G, S, dk)

    nb = S // Q_BLOCK
    span = Q_BLOCK + WINDOW
    band_idx = np.arange(nb)[:, None] * Q_BLOCK + np.arange(span)[None, :]
    pad = ((0, 0), (0, 0), (WINDOW, 0), (0, 0))
    kw = jnp.pad(k_win, pad)[:, :, band_idx]
    vw = jnp.pad(v_win, pad)[:, :, band_idx]
    dist_w = jnp.asarray(WINDOW + np.arange(Q_BLOCK)[:, None] - np.arange(span)[None, :], dtype=jnp.int32)
    key_pos = jnp.asarray(band_idx - WINDOW)
    mask_w = ((dist_w >= 0) & (dist_w < WINDOW))[None] & (key_pos >= 0)[:, None, :]
    bias_w = head_bias(rel_bias, dist_w)[:, :, None]
    qw = q.reshape(B, Hkv, G, nb, Q_BLOCK, dk)
    s_w = jnp.einsum('bhgiqd,bhikd->bhgiqk', qw, kw).astype(jnp.float32) * scale + bias_w
    p_w = masked_softmax(s_w, mask_w)
    o_win = jnp.einsum('bhgiqk,bhikd->bhgiqd', p_w.astype(vw.dtype), vw).reshape(B, Hkv, G, S, dk)

    gt = jax.nn.sigmoid(gates.reshape(B, S, 3, Hkv, G)).transpose(2, 0, 3, 4, 1)[..., None]
    o = gt[0] * o_cmp + gt[1] * o_slc + gt[2] * o_win
    return o.transpose(0, 3, 1, 2, 4).reshape(B, S, H * dk)


def short_conv_mixer(u, b, c, w):
    return b * causal_depthwise_conv(c * u, w)


def setup_inputs(seed: int = 0) -> dict:
    key = jax.random.key(seed)
    ks = jax.random.split(key, 20)
    f32 = jnp.float32

    def nrm(k, shape, fan_in):
        return jax.random.normal(k, shape, f32) * (fan_in ** -0.5)

    def gain(k, shape):
        return 1.0 + 0.01 * jax.random.normal(k, shape, f32)

    x = jax.random.normal(ks[0], (BATCH, SEQ, D_MODEL), f32)
    rel_bias = 0.2 * jax.random.normal(ks[1], (NUM_BUCKETS, NSA_HEADS), f32)
    norm_mix = gain(ks[2], (DEPTH, D_MODEL))
    w_in = nrm(ks[3], (DEPTH, D_MODEL, PROJ_DIM), D_MODEL)
    gdn_conv = nrm(ks[4], (DEPTH, 3 * GDN_DIM, GDN_CONV_W), GDN_CONV_W)
    gdn_a_log = jnp.log(jax.random.uniform(ks[5], (DEPTH, GDN_HEADS), f32, 1.0, 16.0))
    dt = jnp.exp(jax.random.uniform(ks[6], (DEPTH, GDN_HEADS), f32, math.log(1e-3), math.log(1e-1)))
    gdn_dt_bias = dt + jnp.log(-jnp.expm1(-dt))
    gdn_norm = gain(ks[7], (DEPTH, HEAD_DIM))
    nsa_q_norm = gain(ks[8], (DEPTH, HEAD_DIM))
    nsa_k_norm = gain(ks[9], (DEPTH, 3, HEAD_DIM))
    cmp_pos = 0.02 * jax.random.normal(ks[10], (DEPTH, 2, L_CMP, HEAD_DIM), f32)
    cmp_w1 = nrm(ks[11], (DEPTH, 2, L_CMP * HEAD_DIM, HEAD_DIM), L_CMP * HEAD_DIM)
    cmp_w2 = nrm(ks[12], (DEPTH, 2, HEAD_DIM, HEAD_DIM), HEAD_DIM)
    sconv_w = nrm(ks[13], (DEPTH, CONV_DIM, SHORT_CONV_W), SHORT_CONV_W)
    w_out = nrm(ks[14], (DEPTH, D_MODEL, D_MODEL), D_MODEL)
    norm_ffn = gain(ks[15], (DEPTH, D_MODEL))
    w_gate = nrm(ks[16], (DEPTH, D_MODEL, D_FF), D_MODEL)
    w_up = nrm(ks[17], (DEPTH, D_MODEL, D_FF), D_MODEL)
    w_down = nrm(ks[18], (DEPTH, D_FF, D_MODEL), D_FF)
    return {"x": x, "rel_bias": rel_bias, "norm_mix": norm_mix, "w_in": w_in, "gdn_conv": gdn_conv,
            "gdn_a_log": gdn_a_log, "gdn_dt_bias": gdn_dt_bias, "gdn_norm": gdn_norm,
            "nsa_q_norm": nsa_q_norm, "nsa_k_norm": nsa_k_norm, "cmp_pos": cmp_pos, "cmp_w1": cmp_w1,
            "cmp_w2": cmp_w2, "sconv_w": sconv_w, "w_out": w_out, "norm_ffn": norm_ffn,
            "w_gate": w_gate, "w_up": w_up, "w_down": w_down}


def reference(x, rel_bias, norm_mix, w_in, gdn_conv, gdn_a_log, gdn_dt_bias, gdn_norm, nsa_q_norm, nsa_k_norm,
              cmp_pos, cmp_w1, cmp_w2, sconv_w, w_out, norm_ffn, w_gate, w_up, w_down):
    splits = np.cumsum(PROJ_SIZES)[:-1].tolist()
    for l in range(DEPTH):
        h = rms_norm(x, norm_mix[l])
        proj = h @ w_in[l]
        (gdn_qkv, gdn_z, gdn_b, gdn_a, nsa_q, nsa_kv, nsa_g, cu, cb, cc) = jnp.split(proj, splits, axis=-1)
        y_gdn = gdn_mixer(gdn_qkv, gdn_z, gdn_b, gdn_a, gdn_conv[l], gdn_a_log[l], gdn_dt_bias[l], gdn_norm[l])
        y_nsa = nsa_mixer(nsa_q, nsa_kv, nsa_g, nsa_q_norm[l], nsa_k_norm[l], cmp_pos[l], cmp_w1[l], cmp_w2[l], rel_bias)
        y_conv = short_conv_mixer(cu, cb, cc, sconv_w[l])
        x = x + jnp.concatenate([y_gdn, y_nsa, y_conv], axis=-1) @ w_out[l]
        h = rms_norm(x, norm_ffn[l])
        x = x + (jax.nn.silu(h @ w_gate[l]) * (h @ w_up[l])) @ w_down[l]
    return x
```

```python
import math
import numpy as np
import concourse.bass as bass
import concourse.mybir as mybir
from concourse.bass_utils import run_bass_kernel_spmd
from contextlib import ExitStack

F32 = mybir.dt.float32
BF16 = mybir.dt.bfloat16
AF = mybir.ActivationFunctionType
ALU = mybir.AluOpType

D = 2048
SEQ = 2048
DEPTH = 4
PROJ = 6942
DFF = 5632
NH = 6
EPS = 1e-6
NCORES = 8

ENGS = ("pe", "act", "dve", "pool", "sp")
SIG_CAP = 30000


class Res:
    __slots__ = ("writer", "readers", "dsem", "last_dma")

    def __init__(self):
        self.writer = None
        self.readers = {}
        self.dsem = None
        self.last_dma = None


class Op:
    __slots__ = ("eng", "fn", "deps", "needs_sig", "sem", "val", "dma", "done")

    def __init__(self, eng, fn):
        self.eng = eng
        self.fn = fn
        self.deps = ()
        self.needs_sig = False
        self.sem = None
        self.val = 0
        self.dma = False
        self.done = False


class Sched:
    def __init__(self, nc, stack, n_eng_epochs=8, n_dma_sems=62):
        self.nc = nc
        self.eng_sems = {e: [stack.enter_context(nc.semaphore(f"s_{e}{i}")) for i in range(n_eng_epochs)]
                         for e in ENGS if e != "sp"}
        self.dma_sems = [stack.enter_context(nc.semaphore(f"s_dma{i}")) for i in range(n_dma_sems)]
        self.dma_cnt = [0] * n_dma_sems
        self.dma_free = list(range(n_dma_sems))
        self.sig_cnt = {e: 0 for e in ENGS}
        self.waited = {e: {} for e in ENGS}
        self.pending = {e: [] for e in ENGS}
        self.phase_dma_res = []
        self.n_ops = 0

    def op(self, eng, fn, reads=(), writes=(), dma=None):
        o = Op(eng, fn)
        self.n_ops += 1
        deps = {}
        for r in reads:
            if r.writer is not None:
                deps[id(r.writer)] = r.writer
        for w in writes:
            if w.writer is not None:
                deps[id(w.writer)] = w.writer
            for rd in w.readers.values():
                deps[id(rd)] = rd
        if dma is not None:
            o.dma = True
            if dma.dsem is None:
                assert self.dma_free, "out of DMA semaphores in this phase"
                self.dma_free.sort(key=lambda i: self.dma_cnt[i])
                dma.dsem = self.dma_free.pop(0)
                self.phase_dma_res.append(dma)
            if dma.last_dma is not None:
                deps[id(dma.last_dma)] = dma.last_dma
            self.dma_cnt[dma.dsem] += 16
            o.sem = self.dma_sems[dma.dsem]
            o.val = self.dma_cnt[dma.dsem]
            dma.last_dma = o
        dl = []
        for d in deps.values():
            if d is o or d.done:
                continue
            if (not d.dma) and d.eng == "pe" and eng == "pe" and not o.dma:
                continue
            if not d.dma:
                d.needs_sig = True
            dl.append(d)
        o.deps = dl
        for r in reads:
            key = ("dma", id(o)) if o.dma else eng
            r.readers[key] = o
        for w in writes:
            w.writer = o
            w.readers = {}
        self.pending[eng].append(o)
        return o

    def barrier(self):
        tails = []
        for e in ENGS:
            for o in reversed(self.pending[e]):
                if not o.dma and o.fn is not None:
                    tails.append(o)
                    break
        dmas = [r.last_dma for r in self.phase_dma_res if r.last_dma is not None]
        for e in ENGS:
            o = Op(e, None)
            dl = []
            for t in tails:
                if t.eng != e:
                    t.needs_sig = True
                    dl.append(t)
            dl.extend(dmas)
            o.deps = dl
            self.pending[e].append(o)

    def _assign(self):
        for e in ENGS:
            for o in self.pending[e]:
                if o.dma or o.fn is None:
                    continue
                if o.needs_sig:
                    self.sig_cnt[e] += 1
                    c = self.sig_cnt[e]
                    ep = (c - 1) // SIG_CAP
                    o.sem = self.eng_sems[e][ep]
                    o.val = c - ep * SIG_CAP

    def flush(self):
        nc = self.nc
        self._assign()
        pend = self.pending
        sched = self

        def emit(e, engobj):
            waited = sched.waited[e]
            for o in pend[e]:
                for d in o.deps:
                    key = id(d.sem)
                    if waited.get(key, 0) >= d.val:
                        continue
                    engobj.wait_ge(d.sem, d.val)
                    waited[key] = d.val
                if o.fn is None:
                    continue
                ins = o.fn(engobj)
                if o.dma:
                    ins.then_inc(o.sem, 16)
                elif o.needs_sig:
                    ins.then_inc(o.sem, 1)

        with nc.Block() as block:
            @block.tensor
            def _(eng):
                emit("pe", eng)

            @block.scalar
            def _(eng):
                emit("act", eng)

            @block.vector
            def _(eng):
                emit("dve", eng)

            @block.gpsimd
            def _(eng):
                emit("pool", eng)

            @block.sync
            def _(eng):
                emit("sp", eng)

        for e in ENGS:
            for o in self.pending[e]:
                o.done = True
        self.pending = {e: [] for e in ENGS}
        for r in self.phase_dma_res:
            self.dma_free.append(r.dsem)
            r.dsem = None
            r.last_dma = None
        self.phase_dma_res = []

    def end_phase(self):
        self.barrier()
        self.flush()


class Ring:
    def __init__(self, items):
        self.items = items
        self.i = 0

    def next(self):
        it = self.items[self.i % len(self.items)]
        self.i += 1
        return it


R_GQ, R_GK, R_GV, R_NQ, R_KCMP, R_VCMP, R_KSLC, R_KWIN, R_CU, R_CB, R_CC = (
    0, 768, 1536, 2304, 3072, 3328, 3584, 3840, 4096, 4608, 5120)
FT_ROWS = 5632
FT_SEGS = [(0, 2304, 0), (3084, 768, 2304), (3852, 768, 3072), (4876, 256, 3840), (5406, 1536, 4096)]
TM_W = 1310
TM_SEGS = [(2304, 780, 0), (4620, 256, 780), (5132, 274, 1036)]
TM_Z, TM_B, TM_A, TM_VSLC, TM_VWIN, TM_GATE = 0, 768, 774, 780, 1036, 1292


def _blocks(segs, maxw=512):
    out = []
    for c0, n, r0 in segs:
        o = 0
        while o < n:
            w = min(maxw, n - o)
            out.append((c0 + o, w, r0 + o))
            o += w
    return out


FT_BLOCKS = _blocks(FT_SEGS)
TM_BLOCKS = _blocks(TM_SEGS)


class KB:
    def __init__(self, nseq=2, layers=(0, 1, 2, 3), debug=False, parts=("ab", "sconv", "gdn", "nsa", "cd"), dbg_in=()):
        self.nseq = nseq
        self.NT = nseq * SEQ
        self.layers = layers
        self.debug = debug
        self.parts = parts
        self.uid = 0
        nc = self.nc = bass.Bass("TRN2", target_bir_lowering=False)
        NT = self.NT

        def din(name, shape, dt=F32):
            return nc.dram_tensor(name, list(shape), dt, kind="ExternalInput").ap()

        def dscr(name, shape, dt=F32):
            kind = "ExternalOutput" if debug else "Internal"
            if name in dbg_in:
                kind = "ExternalInput"
            return nc.dram_tensor(name, list(shape), dt, kind=kind).ap()

        self.xT = din("xT", [D, NT])
        self.w_in = din("w_in", [DEPTH, D, PROJ])
        self.w_out = din("w_out", [DEPTH, D, D])
        self.w_gate = din("w_gate", [DEPTH, D, DFF])
        self.w_up = din("w_up", [DEPTH, D, DFF])
        self.w_down = din("w_down", [DEPTH, DFF, D])
        self.cmp_w1 = din("cmp_w1", [DEPTH, 2, 4096, 128])
        self.cmp_w2 = din("cmp_w2", [DEPTH, 2, 128, 128])
        self.nmixT = din("nmixT", [DEPTH, 128, 16])
        self.nffnT = din("nffnT", [DEPTH, 128, 16])
        self.gconvT = din("gconvT", [DEPTH, 128, 18, 4])
        self.galog = din("galog", [DEPTH, 128, 6])
        self.gdtb = din("gdtb", [DEPTH, 128, 6])
        self.gnorm_row = din("gnorm_row", [DEPTH, 128, 128])
        self.qnormT = din("qnormT", [DEPTH, 128, 1])
        self.knormT = din("knormT", [DEPTH, 128, 3])
        self.knorm0_row = din("knorm0_row", [DEPTH, 128, 128])
        self.cposT = din("cposT", [DEPTH, 2, 128, 32])
        self.sconvT = din("sconvT", [DEPTH, 128, 4, 3])
        self.relb_rep = din("relb_rep", [128, 192])
        self.c_idxC = din("c_idxC", [127, 2048])
        self.c_idxW = din("c_idxW", [128, 2, 128])
        self.c_maskW0 = din("c_maskW0", [128, 128])
        self.c_keep = din("c_keep", [128, 16, 32])
        self.c_add = din("c_add", [128, 16, 32])
        self.c_E32 = din("c_E32", [32, 2048])
        self.c_ovl = din("c_ovl", [127, 32])
        self.c_triu = din("c_triu", [128, 128])
        self.c_msl = din("c_msl", [128, 128])
        self.c_mui = din("c_mui", [128, 128])
        self.c_ident = din("c_ident", [128, 128])

        self.outT = nc.dram_tensor("outT", [D, NT], F32, kind="ExternalOutput").ap()
        self.XM = dscr("XM", [D, NT])
        self.FT = dscr("FT", [FT_ROWS, SEQ])
        self.TM = dscr("TM", [SEQ, TM_W])
        self.GQ = dscr("GQ", [2304, SEQ])
        self.CT = dscr("CT", [D, SEQ], BF16)
        self.TCd = dscr("TCd", [127, 6 * 2048], BF16)
        self.TWd = dscr("TWd", [128, 2 * 6 * 128], BF16)

        self.top = ExitStack()
        self.S = Sched(nc, self.top)

    def name(self, p):
        self.uid += 1
        return f"{p}{self.uid}"

    def mm(self, out, lhsT, rhs, start, stop, reads, writes):
        return self.S.op("pe", lambda e: e.matmul(out, lhsT=lhsT, rhs=rhs, start=start, stop=stop), reads, writes)

    def tr(self, out, in_, ident, reads, writes):
        return self.S.op("pe", lambda e: e.transpose(out, in_, ident), reads, writes)

    def act(self, out, in_, func, reads, writes, bias=None, scale=None, accum=None):
        kw = {}
        if bias is not None:
            kw["bias"] = bias
        if scale is not None:
            kw["scale"] = scale
        if accum is not None:
            kw["accum_out"] = accum
        return self.S.op("act", lambda e: e.activation(out=out, in_=in_, func=func, **kw), reads, writes)

    def amul(self, out, in_, mul, reads, writes):
        return self.S.op("act", lambda e: e.mul(out=out, in_=in_, mul=mul), reads, writes)

    def tt(self, eng, out, in0, in1, op, reads, writes):
        return self.S.op(eng, lambda e: e.tensor_tensor(out=out, in0=in0, in1=in1, op=op), reads, writes)

    def stt(self, eng, out, in0, scalar, in1, op0, op1, reads, writes):
        return self.S.op(eng, lambda e: e.scalar_tensor_tensor(out=out, in0=in0, scalar=scalar, in1=in1, op0=op0, op1=op1),
                         reads, writes)

    def ts(self, eng, out, in0, s1, s2, op0, op1, reads, writes):
        if s2 is None:
            return self.S.op(eng, lambda e: e.tensor_scalar(out=out, in0=in0, scalar1=s1, scalar2=None, op0=op0), reads, writes)
        return self.S.op(eng, lambda e: e.tensor_scalar(out=out, in0=in0, scalar1=s1, scalar2=s2, op0=op0, op1=op1), reads, writes)

    def cp(self, eng, out, in_, reads, writes):
        if eng == "act":
            return self.S.op("act", lambda e: e.copy(out=out, in_=in_), reads, writes)
        return self.S.op(eng, lambda e: e.tensor_copy(out=out, in_=in_), reads, writes)

    def recip(self, out, in_, reads, writes):
        return self.S.op("dve", lambda e: e.reciprocal(out=out, in_=in_), reads, writes)

    def memset(self, eng, ap, val, writes):
        return self.S.op(eng, lambda e: e.memset(ap, val), (), writes)

    def dma(self, q, out, in_, reads, writes, res):
        return self.S.op(q, lambda e: e.dma_start(out=out, in_=in_), reads, writes, dma=res)


class Phase:
    def __init__(self, kb):
        self.kb = kb
        self.nc = kb.nc
        self.st = ExitStack()

    def sb(self, shape, dt, tag="t"):
        return self.st.enter_context(self.nc.sbuf_tensor(self.kb.name(tag), list(shape), dt))

    def ps(self, shape, dt=F32, tag="p"):
        return self.st.enter_context(self.nc.psum_tensor(self.kb.name(tag), list(shape), dt))

    def banks(self, n=8):
        return [(self.ps([128, 512]), Res()) for _ in range(n)]

    def load_const(self, dram_ap, shape, dt=F32, q="sp"):
        t = self.sb(shape, dt, "c")
        r = Res()
        self.kb.dma(q, t[:], dram_ap, (), [r], r)
        return t, r

    def close(self):
        self.kb.S.end_phase()
        self.st.close()


def phase_ab(kb, l, s, first):
    P = Phase(kb)
    S = kb.S
    xsrc = kb.xT if first else kb.XM
    t0 = s * SEQ
    hT = P.sb([128, 16, SEQ], BF16, "hT")
    r_h = [Res() for _ in range(16)]
    ones = P.sb([128, 128], F32, "ones")
    r_ones = Res()
    kb.memset("pool", ones[:], 1.0, [r_ones])
    g1, r_g1 = P.load_const(kb.nmixT[l], [128, 16])
    xk = Ring([(P.sb([128, SEQ], F32, "xk"), Res()) for _ in range(2)])
    sq = Ring([(P.sb([128, SEQ], F32, "sq"), Res()) for _ in range(2)])
    rstd = P.sb([128, SEQ], F32, "rstd")
    r_rstd = [Res() for _ in range(4)]
    banks = P.banks(8)
    for k in range(16):
        xt, rx = xk.next()
        kb.dma("sp", xt[:], xsrc[k * 128:(k + 1) * 128, t0:t0 + SEQ], (), [rx], rx)
        st, rs = sq.next()
        kb.act(st[:], xt[:], AF.Square, [rx], [rs])
        for n in range(4):
            kb.mm(banks[n][0][:], ones[:], st[:, n * 512:(n + 1) * 512], k == 0, k == 15, [r_ones, rs], [banks[n][1]])
    for n in range(4):
        kb.act(rstd[:, n * 512:(n + 1) * 512], banks[n][0][:], AF.Sqrt, [banks[n][1]], [r_rstd[n]], bias=EPS, scale=1.0 / D)
        kb.recip(rstd[:, n * 512:(n + 1) * 512], rstd[:, n * 512:(n + 1) * 512], [r_rstd[n]], [r_rstd[n]])
    for k in range(16):
        xt, rx = xk.next()
        kb.dma("sp", xt[:], xsrc[k * 128:(k + 1) * 128, t0:t0 + SEQ], (), [rx], rx)
        kb.stt("dve", hT[:, k, :], xt[:], g1[:, k:k + 1], rstd[:], ALU.mult, ALU.mult, [rx, r_g1] + r_rstd, [r_h[k]])
    wring = Ring([(P.sb([128, 16, 512], BF16, "wt"), Res()) for _ in range(2)])
    stage = Ring([(P.sb([128, SEQ], F32, "stg"), [Res() for _ in range(4)], Res()) for _ in range(2)])
    bring = Ring(banks)
    ev = 0
    for (c0, ncol, r0) in FT_BLOCKS:
        wt, rw = wring.next()
        kb.dma("pool", wt[:, :, 0:ncol], kb.w_in[l, :, c0:c0 + ncol].rearrange("(k p) c -> p k c", p=128), (), [rw], rw)
        for m in range(ncol // 128):
            stg, rstg, rdma = stage.next()
            for n in range(4):
                bk, rb = bring.next()
                for k in range(16):
                    kb.mm(bk[:], wt[:, k, m * 128:(m + 1) * 128], hT[:, k, n * 512:(n + 1) * 512], k == 0, k == 15,
                          [rw, r_h[k]], [rb])
                eng = "act" if ev % 2 == 0 else "dve"
                ev += 1
                kb.cp(eng, stg[:, n * 512:(n + 1) * 512], bk[:], [rb], [rstg[n]])
            kb.dma("sp", kb.FT[r0 + m * 128:r0 + (m + 1) * 128, :], stg[:], rstg, (), rdma)
    stage2 = Ring([(P.sb([128, 512], F32, "stg2"), Res()) for _ in range(3)])
    for (c0, ncol, tc0) in TM_BLOCKS:
        wt, rw = wring.next()
        kb.dma("pool", wt[:, :, 0:ncol], kb.w_in[l, :, c0:c0 + ncol].rearrange("(k p) c -> p k c", p=128), (), [rw], rw)
        for t in range(16):
            bk, rb = bring.next()
            for k in range(16):
                kb.mm(bk[:, 0:ncol], hT[:, k, t * 128:(t + 1) * 128], wt[:, k, 0:ncol], k == 0, k == 15, [rw, r_h[k]], [rb])
            stg, rs = stage2.next()
            eng = "act" if ev % 2 == 0 else "dve"
            ev += 1
            kb.cp(eng, stg[:, 0:ncol], bk[:, 0:ncol], [rb], [rs])
            kb.dma("sp", kb.TM[t * 128:(t + 1) * 128, tc0:tc0 + ncol], stg[:, 0:ncol], [rs], (), rs)
    P.close()


def phase_sconv(kb, l):
    P = Phase(kb)
    w, rw = P.load_const(kb.sconvT[l], [128, 4, 3])
    ring = Ring([[(P.sb([128, SEQ], F32, "sc"), Res()) for _ in range(3)] for _ in range(2)])
    vbuf = Ring([(P.sb([128, SEQ], F32, "scv"), Res()) for _ in range(2)])
    ybuf = Ring([(P.sb([128, SEQ], F32, "scy"), Res()) for _ in range(2)])
    obuf = Ring([(P.sb([128, SEQ], BF16, "sco"), Res()) for _ in range(2)])
    for g in range(4):
        (cu, rcu), (cb, rcb), (cc, rcc) = ring.next()
        kb.dma("sp", cu[:], kb.FT[R_CU + g * 128:R_CU + (g + 1) * 128, :], (), [rcu], rcu)
        kb.dma("sp", cb[:], kb.FT[R_CB + g * 128:R_CB + (g + 1) * 128, :], (), [rcb], rcb)
        kb.dma("sp", cc[:], kb.FT[R_CC + g * 128:R_CC + (g + 1) * 128, :], (), [rcc], rcc)
        v, rv = vbuf.next()
        y, ry = ybuf.next()
        o, ro = obuf.next()
        kb.tt("pool", v[:], cc[:], cu[:], ALU.mult, [rcc, rcu], [rv])
        kb.ts("dve", y[:], v[:], w[:, g, 2:3], None, ALU.mult, None, [rv, rw], [ry])
        kb.stt("dve", y[:, 1:SEQ], v[:, 0:SEQ - 1], w[:, g, 1:2], y[:, 1:SEQ], ALU.mult, ALU.add, [rv, rw, ry], [ry])
        kb.stt("dve", y[:, 2:SEQ], v[:, 0:SEQ - 2], w[:, g, 0:1], y[:, 2:SEQ], ALU.mult, ALU.add, [rv, rw, ry], [ry])
        kb.tt("pool", o[:], cb[:], y[:], ALU.mult, [rcb, ry], [ro])
        kb.dma("sp", kb.CT[1536 + g * 128:1536 + (g + 1) * 128, :], o[:], [ro], (), ro)
    P.close()


def phase_cd(kb, l, s, nb, first, last):
    P = Phase(kb)
    xsrc = kb.xT if first else kb.XM
    xdst = kb.outT if last else kb.XM
    t0 = s * SEQ + nb * 512
    c0 = nb * 512
    act = P.sb([128, 44, 512], BF16, "act")
    r_act = [Res() for _ in range(44)]
    xmid = P.sb([128, 16, 512], F32, "xmid")
    r_x = [Res() for _ in range(16)]
    h2 = P.sb([128, 16, 512], BF16, "h2")
    r_h2 = [Res() for _ in range(16)]
    ones = P.sb([128, 128], F32, "ones")
    r_ones = Res()
    kb.memset("pool", ones[:], 1.0, [r_ones])
    g2, r_g2 = P.load_const(kb.nffnT[l], [128, 16])
    rstd = P.sb([128, 512], F32, "rstd")
    r_rstd = Res()
    sqr = Ring([(P.sb([128, 512], F32, "sq"), Res()) for _ in range(2)])
    sgr = Ring([(P.sb([128, 512], F32, "sg"), Res()) for _ in range(2)])
    wring = Ring([(P.sb([128, 16, 512], BF16, "wt"), Res()) for _ in range(3)])
    wdring = Ring([(P.sb([128, 44, 256], BF16, "wd"), Res()) for _ in range(2)])
    banks = P.banks(8)
    bring = Ring(banks[0:7])
    ssb, r_ssb = banks[7]
    for k in range(16):
        kb.dma("sp", act[:, k, :], kb.CT[k * 128:(k + 1) * 128, c0:c0 + 512], (), [r_act[k]], r_act[k])
    for m in range(16):
        kb.dma("sp", xmid[:, m, :], xsrc[m * 128:(m + 1) * 128, t0:t0 + 512], (), [r_x[m]], r_x[m])
    for mb in range(4):
        wt, rw = wring.next()
        kb.dma("pool", wt[:], kb.w_out[l, :, mb * 512:(mb + 1) * 512].rearrange("(k p) c -> p k c", p=128), (), [rw], rw)
        for mi in range(4):
            m = mb * 4 + mi
            bk, rb = bring.next()
            for k in range(16):
                kb.mm(bk[:], wt[:, k, mi * 128:(mi + 1) * 128], act[:, k, :], k == 0, k == 15, [rw, r_act[k]], [rb])
            kb.tt("dve", xmid[:, m, :], xmid[:, m, :], bk[:], ALU.add, [r_x[m], rb], [r_x[m]])
            sq, rs = sqr.next()
            kb.act(sq[:], xmid[:, m, :], AF.Square, [r_x[m]], [rs])
            kb.mm(ssb[:], ones[:], sq[:], m == 0, m == 15, [r_ones, rs], [r_ssb])
    kb.act(rstd[:], ssb[:], AF.Sqrt, [r_ssb], [r_rstd], bias=EPS, scale=1.0 / D)
    kb.recip(rstd[:], rstd[:], [r_rstd], [r_rstd])
    for m in range(16):
        kb.stt("dve", h2[:, m, :], xmid[:, m, :], g2[:, m:m + 1], rstd[:], ALU.mult, ALU.mult, [r_x[m], r_g2, r_rstd], [r_h2[m]])
    for fb in range(11):
        wg, rwg = wring.next()
        kb.dma("pool", wg[:], kb.w_gate[l, :, fb * 512:(fb + 1) * 512].rearrange("(k p) c -> p k c", p=128), (), [rwg], rwg)
        wu, rwu = wring.next()
        kb.dma("pool", wu[:], kb.w_up[l, :, fb * 512:(fb + 1) * 512].rearrange("(k p) c -> p k c", p=128), (), [rwu], rwu)
        for fi in range(4):
            f = fb * 4 + fi
            bg, rbg = bring.next()
            for k in range(16):
                kb.mm(bg[:], wg[:, k, fi * 128:(fi + 1) * 128], h2[:, k, :], k == 0, k == 15, [rwg, r_h2[k]], [rbg])
            bu, rbu = bring.next()
            for k in range(16):
                kb.mm(bu[:], wu[:, k, fi * 128:(fi + 1) * 128], h2[:, k, :], k == 0, k == 15, [rwu, r_h2[k]], [rbu])
            sg, rsg = sgr.next()
            kb.act(sg[:], bg[:], AF.Silu, [rbg], [rsg])
            kb.tt("dve", act[:, f, :], sg[:], bu[:], ALU.mult, [rsg, rbu], [r_act[f]])
    for mb in range(8):
        wd, rwd = wdring.next()
        kb.dma("pool", wd[:], kb.w_down[l, :, mb * 256:(mb + 1) * 256].rearrange("(k p) c -> p k c", p=128), (), [rwd], rwd)
        for mi in range(2):
            m = mb * 2 + mi
            bk, rb = bring.next()
            for f in range(44):
                kb.mm(bk[:], wd[:, f, mi * 128:(mi + 1) * 128], act[:, f, :], f == 0, f == 43, [rwd, r_act[f]], [rb])
            kb.tt("dve", xmid[:, m, :], xmid[:, m, :], bk[:], ALU.add, [r_x[m], rb], [r_x[m]])
            kb.dma("sp", xdst[m * 128:(m + 1) * 128, t0:t0 + 512], xmid[:, m, :], [r_x[m]], (), r_x[m])
    P.close()


def zero_ct_rows(kb, r0, r1):
    P = Phase(kb)
    z = P.sb([128, SEQ], BF16, "z")
    rz = Res()
    kb.memset("pool", z[:], 0.0, [rz])
    for r in range(r0, r1, 128):
        kb.dma("sp", kb.CT[r:r + 128, :], z[:], [rz], (), rz)
    P.close()


def _t5_bucket_np(dist):
    n = np.maximum(dist, 0)
    nf = np.maximum(n, 1).astype(np.float32)
    large = 16 + (np.log(nf / np.float32(16)) / np.float32(math.log(8.0)) * np.float32(16)).astype(np.int32)
    large = np.minimum(large, 31)
    return np.where(n < 16, n, large)


def position_constants():
    c = {}
    t = np.arange(2048)
    n = np.arange(127)
    dist = t[None, :] - (16 * n[:, None] + 31)
    c["c_idxC"] = np.where(dist >= 0, _t5_bucket_np(dist), -1).astype(np.float32)
    kl = np.arange(128)[:, None]
    ql = np.arange(128)[None, :]
    d3 = 128 + ql - kl
    d4 = ql - kl
    idxW = np.stack([_t5_bucket_np(d3), np.where(d4 >= 0, _t5_bucket_np(d4), -1)], axis=1)
    c["c_idxW"] = idxW.astype(np.float32)
    c["c_maskW0"] = (ql < kl).astype(np.float32)
    i = np.arange(16)[None, :, None]
    q = np.arange(128)[:, None, None]
    j = np.arange(32)[None, None, :]
    cur = (128 * i + q) // 64
    future = j > cur
    forced = (j == 0) | (j == cur) | (j == cur - 1)
    c["c_keep"] = (~future & ~forced).astype(np.float32)
    c["c_add"] = np.where(future, -1.0, np.where(forced, 1e4, 0.0)).astype(np.float32)
    c["c_E32"] = (np.arange(2048)[None, :] // 64 == np.arange(32)[:, None]).astype(np.float32)
    cs = 16 * np.arange(127)[:, None]
    ss = 64 * np.arange(32)[None, :]
    c["c_ovl"] = ((cs < ss + 64) & (cs + 32 > ss)).astype(np.float32)
    p = np.arange(128)[:, None]
    f = np.arange(128)[None, :]
    c["c_triu"] = (p <= f).astype(np.float32)
    c["c_msl"] = np.where(p > f, 0.0, -1e9).astype(np.float32)
    c["c_mui"] = np.where(f >= p, 0.0, 1e9).astype(np.float32)
    c["c_ident"] = np.eye(128, dtype=np.float32)
    return c


def shared_inputs(inp):
    f = lambda a: np.ascontiguousarray(np.asarray(a, dtype=np.float32))
    sh = {}
    for k in ("w_in", "w_out", "w_gate", "w_up", "w_down", "cmp_w1", "cmp_w2"):
        sh[k] = f(inp[k])
    sh["nmixT"] = f(np.asarray(inp["norm_mix"]).reshape(DEPTH, 16, 128).transpose(0, 2, 1))
    sh["nffnT"] = f(np.asarray(inp["norm_ffn"]).reshape(DEPTH, 16, 128).transpose(0, 2, 1))
    sh["gconvT"] = f(np.asarray(inp["gdn_conv"]).reshape(DEPTH, 18, 128, 4).transpose(0, 2, 1, 3))
    sh["galog"] = f(np.broadcast_to(np.asarray(inp["gdn_a_log"])[:, None, :], (DEPTH, 128, 6)))
    sh["gdtb"] = f(np.broadcast_to(np.asarray(inp["gdn_dt_bias"])[:, None, :], (DEPTH, 128, 6)))
    sh["gnorm_row"] = f(np.broadcast_to(np.asarray(inp["gdn_norm"])[:, None, :], (DEPTH, 128, 128)))
    sh["qnormT"] = f(np.asarray(inp["nsa_q_norm"]).reshape(DEPTH, 128, 1))
    sh["knormT"] = f(np.asarray(inp["nsa_k_norm"]).transpose(0, 2, 1))
    sh["knorm0_row"] = f(np.broadcast_to(np.asarray(inp["nsa_k_norm"])[:, 0][:, None, :], (DEPTH, 128, 128)))
    sh["cposT"] = f(np.asarray(inp["cmp_pos"]).transpose(0, 1, 3, 2))
    sh["sconvT"] = f(np.asarray(inp["sconv_w"]).reshape(DEPTH, 4, 128, 3).transpose(0, 2, 1, 3))
    sh["relb_rep"] = f(np.broadcast_to(np.asarray(inp["rel_bias"]).reshape(1, 192), (128, 192)))
    sh.update(position_constants())
    return sh


def phase_g1(kb, l):
    P = Phase(kb)
    cw, r_cw = P.load_const(kb.gconvT[l], [128, 18, 4])
    ones = P.sb([128, 128], F32, "ones")
    r_ones = Res()
    kb.memset("pool", ones[:], 1.0, [r_ones])
    raw = Ring([(P.sb([128, SEQ], F32, "raw"), Res()) for _ in range(2)])
    cbr = Ring([(P.sb([128, SEQ], F32, "cb"), Res()) for _ in range(3)])
    sqr = Ring([(P.sb([128, SEQ], F32, "sq"), Res()) for _ in range(2)])
    rnr = Ring([(P.sb([128, SEQ], F32, "rn"), Res()) for _ in range(2)])
    banks = P.banks(8)
    bgrp = Ring([banks[0:4], banks[4:8]])
    for c in range(18):
        x, rx = raw.next()
        kb.dma("sp", x[:], kb.FT[c * 128:(c + 1) * 128, :], (), [rx], rx)
        cb, rc = cbr.next()
        kb.ts("dve", cb[:], x[:], cw[:, c, 3:4], None, ALU.mult, None, [rx, r_cw], [rc])
        for j in (1, 2, 3):
            kb.stt("dve", cb[:, j:SEQ], x[:, 0:SEQ - j], cw[:, c, 3 - j:4 - j], cb[:, j:SEQ], ALU.mult, ALU.add,
                   [rx, r_cw, rc], [rc])
        kb.act(cb[:], cb[:], AF.Silu, [rc], [rc])
        if c < 12:
            sq, rs = sqr.next()
            rn, rr = rnr.next()
            kb.tt("pool", sq[:], cb[:], cb[:], ALU.mult, [rc], [rs])
            bg = bgrp.next()
            for n in range(4):
                kb.mm(bg[n][0][:], ones[:], sq[:, n * 512:(n + 1) * 512], True, True, [r_ones, rs], [bg[n][1]])
            for n in range(4):
                kb.act(rn[:, n * 512:(n + 1) * 512], bg[n][0][:], AF.Sqrt, [bg[n][1]], [rr], bias=EPS, scale=1.0)
            kb.recip(rn[:], rn[:], [rr], [rr])
            if c < 6:
                kb.stt("dve", cb[:], cb[:], 128.0 ** -0.5, rn[:], ALU.mult, ALU.mult, [rc, rr], [rc])
            else:
                kb.tt("pool", cb[:], cb[:], rn[:], ALU.mult, [rc, rr], [rc])
        kb.dma("sp", kb.GQ[c * 128:(c + 1) * 128, :], cb[:], [rc], (), rc)
    P.close()


def _lockstep(gens, maxrounds=None):
    gens = list(gens)
    rounds = 0
    while gens:
        if maxrounds is not None and rounds >= maxrounds:
            break
        rounds += 1
        nxt = []
        for g in gens:
            try:
                next(g)
                nxt.append(g)
            except StopIteration:
                pass
        gens = nxt


def phase_g2(kb, l, stop=None, maxrounds=None):
    P = Phase(kb)
    ident, r_id = P.load_const(kb.c_ident, [128, 128])
    triu, r_triu = P.load_const(kb.c_triu, [128, 128])
    msl, r_msl = P.load_const(kb.c_msl, [128, 128])
    mui, r_mui = P.load_const(kb.c_mui, [128, 128])
    grow, r_grow = P.load_const(kb.gnorm_row[l], [128, 128])
    alog, r_alog = P.load_const(kb.galog[l], [128, 6])
    dtb, r_dtb = P.load_const(kb.gdtb[l], [128, 6])
    ones = P.sb([128, 128], F32, "ones")
    negones = P.sb([128, 128], F32, "nones")
    r_ones, r_nones = Res(), Res()
    kb.memset("pool", ones[:], 1.0, [r_ones])
    kb.memset("pool", negones[:], -1.0, [r_nones])
    banks = P.banks(8)
    slots = [Ring([(banks[h][0][:, q * 128:(q + 1) * 128], banks[h][1]) for q in range(4)]) for h in range(6)]
    ba = P.sb([128, 16, 12], F32, "ba")
    r_ba = Res()
    kb.dma("sp", ba[:], kb.TM[:, TM_B:TM_B + 12].rearrange("(t p) c -> p t c", p=128), (), [r_ba], r_ba)

    def small(tag):
        return P.sb([128, 16, 6], F32, tag), Res()
    beta, r_beta = small("beta")
    nbeta, r_nbeta = small("nbeta")
    gg, r_g = small("g")
    gc, r_gc = small("gc")
    gl, r_gl = small("gl")
    bgs, r_bg = small("bg")
    kds, r_kds = small("kds")
    egl, r_egl = small("egl")
    nal = P.sb([128, 6], F32, "nal")
    r_nal = Res()
    kb.act(beta[:], ba[:, :, 0:6], AF.Sigmoid, [r_ba], [r_beta])
    kb.ts("dve", nbeta[:], beta[:], -1.0, None, ALU.mult, None, [r_beta], [r_nbeta])
    kb.tt("dve", gg[:], ba[:, :, 6:12], dtb[:].unsqueeze(1).to_broadcast([128, 16, 6]), ALU.add, [r_ba, r_dtb], [r_g])
    kb.act(gg[:], gg[:], AF.Exp, [r_g], [r_g])
    kb.act(gg[:], gg[:], AF.Ln, [r_g], [r_g], bias=1.0, scale=1.0)
    kb.act(nal[:], alog[:], AF.Exp, [r_alog], [r_nal])
    kb.ts("dve", nal[:], nal[:], -1.0, None, ALU.mult, None, [r_nal], [r_nal])
    kb.tt("dve", gg[:], gg[:], nal[:].unsqueeze(1).to_broadcast([128, 16, 6]), ALU.mult, [r_g, r_nal], [r_g])
    s0, rs0 = banks[6]
    s4, rs4 = banks[7]
    for i in range(16):
        kb.mm(s0[:, i * 6:(i + 1) * 6], triu[:], gg[:, i, :], True, True, [r_triu, r_g], [rs0])
        kb.mm(s4[:, i * 6:(i + 1) * 6], ones[:], gg[:, i, :], True, True, [r_ones, r_g], [rs4])
    kb.cp("dve", gc[:].rearrange("p t h -> p (t h)"), s0[:, 0:96], [rs0], [r_gc, rs0])
    kb.cp("dve", gl[:].rearrange("p t h -> p (t h)"), s4[:, 0:96], [rs4], [r_gl, rs4])
    kb.act(bgs[:], gc[:], AF.Exp, [r_gc], [r_bg])
    kb.tt("dve", bgs[:], bgs[:], beta[:], ALU.mult, [r_bg, r_beta], [r_bg])
    kb.tt("dve", kds[:], gl[:], gc[:], ALU.subtract, [r_gl, r_gc], [r_kds])
    kb.act(kds[:], kds[:], AF.Exp, [r_kds], [r_kds])
    kb.act(egl[:], gl[:], AF.Exp, [r_gl], [r_egl])
    if stop == "prologue":
        dbg = P.sb([128, 16, 6 * 5], F32, "dbg")
        rd = Res()
        for j, (t_, r_) in enumerate([(gc, r_gc), (gl, r_gl), (bgs, r_bg), (kds, r_kds), (egl, r_egl)]):
            kb.cp("dve", dbg[:, :, j * 6:(j + 1) * 6], t_[:], [r_], [rd])
        kb.dma("sp", kb.XM[:, 0:30].rearrange("(t p) c -> p t c", p=128), dbg[:], [rd], (), rd)
        P.close()
        return
    names1 = ["TriG", "kbg", "vb", "t1", "t2", "EG", "Ds", "DT", "M", "MTs", "TT", "Pa", "PTa", "Pb", "PTb",
              "vnew", "y", "y2", "junk", "S"]
    names2 = ["kdec", "attnT", "qgT", "wT", "u"]
    T = []
    for h in range(6):
        d = {}
        for n in names1:
            d[n] = (P.sb([128, 128], F32, n), Res())
        for n in names2:
            d[n] = [(P.sb([128, 128], F32, n), Res()) for _ in range(2)]
        d["ss"] = (P.sb([128, 1], F32, "ss"), Res())
        d["rs"] = (P.sb([128, 1], F32, "rs"), Res())
        T.append(d)
        kb.memset("pool", d["S"][0][:], 0.0, [d["S"][1]])
    ctst = P.sb([128, 6, SEQ], BF16, "ctst")
    r_ct = [Res() for _ in range(6)]
    xtr = [(P.sb([128, 18, 128], F32, "xt"), Res()) for _ in range(2)]
    ztr = [(P.sb([128, 768], F32, "zt"), Res()) for _ in range(2)]

    def load_tile(i):
        xt, rxt = xtr[i % 2]
        kb.dma("sp", xt[:], kb.GQ[:, i * 128:(i + 1) * 128].rearrange("(c p) t -> p c t", p=128), (), [rxt], rxt)

    def load_z(i):
        zt, rzt = ztr[i % 2]
        kb.dma("sp", zt[:], kb.TM[i * 128:(i + 1) * 128, 0:768], (), [rzt], rzt)
        kb.act(zt[:], zt[:], AF.Silu, [rzt], [rzt])

    def par_gen(h, i):
        d = T[h]
        sl = slots[h]
        par = i % 2
        xt, rxt = xtr[par]
        qn, kn, vT = xt[:, h, :], xt[:, 6 + h, :], xt[:, 12 + h, :]
        sc = lambda t: t[:, i, h:h + 1]
        pk, rpk = sl.next()
        kb.tr(pk, kn, ident[:], [rxt, r_id], [rpk])
        pv, rpv = sl.next()
        kb.tr(pv, vT, ident[:], [rxt, r_id], [rpv])
        TriG, rTriG = d["TriG"]
        kb.ts("dve", TriG[:], triu[:], sc(gg), None, ALU.mult, None, [r_triu, r_g], [rTriG])
        yield
        pG, rpG = sl.next()
        kb.mm(pG, negones[:], TriG[:], True, True, [r_nones, rTriG], [rpG])
        pKK, rpKK = sl.next()
        kb.mm(pKK, kn, kn, True, True, [rxt], [rpKK])
        kdec, rkdec = d["kdec"][par]
        kb.ts("dve", kdec[:], pk, sc(kds), None, ALU.mult, None, [rpk, r_kds], [rkdec, rpk])
        vb, rvb = d["vb"]
        kb.ts("dve", vb[:], pv, sc(beta), None, ALU.mult, None, [rpv, r_beta], [rvb, rpv])
        kbg, rkbg = d["kbg"]
        kb.ts("dve", kbg[:], pk, sc(bgs), None, ALU.mult, None, [rpk, r_bg], [rkbg, rpk])
        yield
        t1, rt1 = d["t1"]
        t2, rt2 = d["t2"]
        EG, rEG = d["EG"]
        kb.stt("dve", t1[:], pG, sc(gc), msl[:], ALU.add, ALU.add, [rpG, r_gc, r_msl], [rt1, rpG])
        kb.stt("dve", t2[:], pG, sc(gc), mui[:], ALU.add, ALU.add, [rpG, r_gc, r_mui], [rt2, rpG])
        kb.act(EG[:], pG, AF.Exp, [rpG], [rEG, rpG], scale=-1.0)
        yield
        Ds, rDs = d["Ds"]
        DT, rDT = d["DT"]
        kb.act(Ds[:], t1[:], AF.Exp, [rt1], [rDs])
        kb.act(DT[:], t2[:], AF.Exp, [rt2], [rDT], scale=-1.0)
        pKQ, rpKQ = sl.next()
        kb.mm(pKQ, kn, qn, True, True, [rxt], [rpKQ])
        yield
        M, rM = d["M"]
        attnT, rattnT = d["attnT"][par]
        qgT, rqgT = d["qgT"][par]
        kb.stt("dve", M[:], pKK, sc(nbeta), Ds[:], ALU.mult, ALU.mult, [rpKK, r_nbeta, rDs], [rM, rpKK])
        kb.tt("dve", attnT[:], pKQ, DT[:], ALU.mult, [rpKQ, rDT], [rattnT, rpKQ])
        kb.tt("pool", qgT[:], qn, EG[:], ALU.mult, [rxt, rEG], [rqgT])
        yield
        pMT, rpMT = sl.next()
        kb.tr(pMT, M[:], ident[:], [rM, r_id], [rpMT])
        yield
        MTs, rMTs = d["MTs"]
        TT, rTT = d["TT"]
        kb.cp("act", MTs[:], pMT, [rpMT], [rMTs, rpMT])
        kb.tt("dve", TT[:], pMT, ident[:], ALU.add, [rpMT, r_id], [rTT, rpMT])
        yield
        Pc, rPc = M, rM
        PTc, rPTc = MTs, rMTs
        bufs = [(d["Pa"], d["PTa"]), (d["Pb"], d["PTb"])]
        for lev in range(1, 7):
            (Pn, rPn), (PTn, rPTn) = bufs[lev % 2]
            pP, rpP = sl.next()
            kb.mm(pP, PTc[:], Pc[:], True, True, [rPTc, rPc], [rpP])
            if lev < 6:
                pPT, rpPT = sl.next()
                kb.mm(pPT, Pc[:], PTc[:], True, True, [rPTc, rPc], [rpPT])
            yield
            kb.cp("act", Pn[:], pP, [rpP], [rPn, rpP])
            if lev < 6:
                kb.cp("dve", PTn[:], pPT, [rpPT], [rPTn, rpPT])
            yield
            pU, rpU = sl.next()
            kb.mm(pU, Pn[:], TT[:], True, True, [rPn, rTT], [rpU])
            yield
            kb.tt("dve", TT[:], TT[:], pU, ALU.add, [rTT, rpU], [rTT, rpU])
            yield
            Pc, rPc, PTc, rPTc = Pn, rPn, PTn, rPTn
        pW, rpW = sl.next()
        kb.mm(pW, kbg[:], TT[:], True, True, [rkbg, rTT], [rpW])
        pu, rpu = sl.next()
        kb.mm(pu, TT[:], vb[:], True, True, [rTT, rvb], [rpu])
        yield
        wT, rwT = d["wT"][par]
        u, ru = d["u"][par]
        kb.cp("act", wT[:], pW, [rpW], [rwT, rpW])
        kb.cp("dve", u[:], pu, [rpu], [ru, rpu])
        yield

    def seq_gen(h, i):
        d = T[h]
        sl = slots[h]
        par = i % 2
        sc = lambda t: t[:, i, h:h + 1]
        Ss, rS = d["S"]
        wT, rwT = d["wT"][par]
        u, ru = d["u"][par]
        qgT, rqgT = d["qgT"][par]
        attnT, rattnT = d["attnT"][par]
        kdec, rkdec = d["kdec"][par]
        vnew, rvn = d["vnew"]
        y, ry = d["y"]
        y2, ry2 = d["y2"]
        junk, rj = d["junk"]
        ss, rss = d["ss"]
        rs, rrs = d["rs"]
        zt, rzt = ztr[par]
        pwS, rpwS = sl.next()
        kb.mm(pwS, wT[:], Ss[:], True, True, [rwT, rS], [rpwS])
        kb.memset("pool", ss[:], 0.0, [rss])
        yield
        kb.tt("dve", vnew[:], u[:], pwS, ALU.subtract, [ru, rpwS], [rvn, rpwS])
        yield
        po, rpo = sl.next()
        kb.mm(po, qgT[:], Ss[:], True, False, [rqgT, rS], [rpo])
        kb.mm(po, attnT[:], vnew[:], False, True, [rattnT, rvn], [rpo])
        pkv, rpkv = sl.next()
        kb.mm(pkv, kdec[:], vnew[:], True, True, [rkdec, rvn], [rpkv])
        yield
        kb.stt("dve", Ss[:], Ss[:], sc(egl), pkv, ALU.mult, ALU.add, [rS, r_egl, rpkv], [rS, rpkv])
        kb.act(junk[:], po, AF.Square, [rpo, rss], [rj, rss, rpo], accum=ss[:])
        yield
        kb.act(rs[:], ss[:], AF.Sqrt, [rss], [rrs], bias=EPS, scale=1.0 / 128)
        yield
        kb.recip(rs[:], rs[:], [rrs], [rrs])
        yield
        kb.stt("dve", y[:], po, rs[:, 0:1], grow[:], ALU.mult, ALU.mult, [rpo, rrs, r_grow], [ry, rpo])
        yield
        kb.tt("pool", y2[:], y[:], zt[:, h * 128:(h + 1) * 128], ALU.mult, [ry, rzt], [ry2])
        yield
        pyT, rpyT = sl.next()
        kb.tr(pyT, y2[:], ident[:], [ry2, r_id], [rpyT])
        yield
        kb.cp("act", ctst[:, h, i * 128:(i + 1) * 128], pyT, [rpyT], [r_ct[h], rpyT])
        yield

    load_tile(0)
    _lockstep([par_gen(h, 0) for h in range(6)], maxrounds)
    if stop == "par0":
        for h in range(6):
            for j, nm in enumerate(["wT", "u", "attnT", "qgT", "kdec"]):
                t_, r_ = T[h][nm][0]
                kb.cp("act", ctst[:, h, j * 128:(j + 1) * 128], t_[:], [r_], [r_ct[h]])
            kb.dma("sp", kb.CT[h * 128:(h + 1) * 128, :], ctst[:, h, :], [r_ct[h]], (), r_ct[h])
        P.close()
        return
    for i in range(16):
        load_z(i)
        gens = [seq_gen(h, i) for h in range(6)]
        if i + 1 < 16:
            load_tile(i + 1)
            gens = gens + [par_gen(h, i + 1) for h in range(6)]
        _lockstep(gens)
    for h in range(6):
        kb.dma("sp", kb.CT[h * 128:(h + 1) * 128, :], ctst[:, h, :], [r_ct[h]], (), r_ct[h])
    P.close()


def phase_tables(kb):
    P = Phase(kb)
    rb, r_rb = P.load_const(kb.relb_rep, [128, 192])
    E = P.sb([128, 32, 6], F32, "E")
    r_E = Res()
    kb.tt("dve", E[:], rb[:].rearrange("p (b h) -> p b h", h=6),
          rb[:, 186:192].unsqueeze(1).to_broadcast([128, 32, 6]), ALU.subtract, [r_rb], [r_E])
    kb.act(E[:], E[:], AF.Exp, [r_E], [r_E])
    idxC, r_ic = P.load_const(kb.c_idxC, [127, 2048])
    idxW, r_iw = P.load_const(kb.c_idxW, [128, 2, 128])
    TC = P.sb([127, 6, 2048], F32, "TC")
    r_TC = [Res() for _ in range(6)]
    TW = P.sb([128, 2, 6, 128], F32, "TW")
    r_TW = [Res() for _ in range(6)]
    for h in range(6):
        kb.memset("pool", TC[:, h, :], 0.0, [r_TC[h]])
        kb.memset("pool", TW[:, :, h, :], 0.0, [r_TW[h]])
    mk = Ring([(P.sb([127, 2048], F32, "mk"), Res()) for _ in range(2)])
    mw = Ring([(P.sb([128, 2, 128], F32, "mw"), Res()) for _ in range(2)])
    for b in range(32):
        m, rm = mk.next()
        kb.ts("pool", m[:], idxC[:], float(b), None, ALU.is_equal, None, [r_ic], [rm])
        m2, rm2 = mw.next()
        kb.ts("pool", m2[:], idxW[:], float(b), None, ALU.is_equal, None, [r_iw], [rm2])
        for h in range(6):
            kb.stt("dve", TC[:, h, :], m[:], E[0:127, b, h:h + 1], TC[:, h, :], ALU.mult, ALU.add, [rm, r_E, r_TC[h]], [r_TC[h]])
            kb.stt("dve", TW[:, :, h, :], m2[:], E[:, b, h:h + 1], TW[:, :, h, :], ALU.mult, ALU.add, [rm2, r_E, r_TW[h]], [r_TW[h]])
    TCb = P.sb([127, 6, 2048], BF16, "TCb")
    TWb = P.sb([128, 2, 6, 128], BF16, "TWb")
    r_b = Res()
    r_b2 = Res()
    for h in range(6):
        kb.cp("act", TCb[:, h, :], TC[:, h, :], [r_TC[h]], [r_b])
        kb.cp("act", TWb[:, :, h, :], TW[:, :, h, :], [r_TW[h]], [r_b2])
    kb.dma("sp", kb.TCd, TCb[:].rearrange("p h t -> p (h t)"), [r_b], (), r_b)
    kb.dma("sp", kb.TWd, TWb[:].rearrange("p c h t -> p (c h t)"), [r_b2], (), r_b2)
    P.close()


def phase_nsa(kb, l, hk):
    P = Phase(kb)
    SC = 128.0 ** -0.5
    ident, r_id = P.load_const(kb.c_ident, [128, 128])
    ones = P.sb([128, 128], F32, "ones")
    r_ones = Res()
    kb.memset("pool", ones[:], 1.0, [r_ones])
    qn, r_qn = P.load_const(kb.qnormT[l], [128, 1])
    kn, r_kn = P.load_const(kb.knormT[l], [128, 3])
    k0row, r_k0 = P.load_const(kb.knorm0_row[l], [128, 128])
    keep, r_keep = P.load_const(kb.c_keep, [128, 16, 32])
    addt, r_add = P.load_const(kb.c_add, [128, 16, 32])
    E32, r_E32 = P.load_const(kb.c_E32, [32, 2048], BF16, q="pool")
    mW0, r_mW0 = P.load_const(kb.c_maskW0, [128, 128], BF16, q="pool")
    TCb, r_TCb = P.load_const(kb.TCd[:, hk * 3 * 2048:(hk * 3 + 3) * 2048].rearrange("p (h t) -> p h t", h=3), [127, 3, 2048], BF16)
    TWb, r_TWb = P.load_const(kb.TWd.rearrange("p (c h t) -> p c h t", c=2, h=6)[:, :, hk * 3:hk * 3 + 3, :], [128, 2, 3, 128], BF16)
    gt, r_gt = P.load_const(kb.TM[:, TM_GATE:TM_GATE + 18].rearrange("(c p) g -> p c g", p=128), [128, 16, 18])
    kb.act(gt[:], gt[:], AF.Sigmoid, [r_gt], [r_gt])
    banks = P.banks(8)
    qT3 = P.sb([128, 3, SEQ], BF16, "qT3")
    r_q3 = Res()
    kslcT = P.sb([128, SEQ], BF16, "kslcT")
    kwinT = P.sb([128, SEQ], BF16, "kwinT")
    r_ks, r_kw = Res(), Res()
    raw = Ring([(P.sb([128, SEQ], F32, "raw"), Res()) for _ in range(2)])
    sqr = Ring([(P.sb([128, SEQ], F32, "sq"), Res()) for _ in range(2)])
    rnr = Ring([(P.sb([128, SEQ], F32, "rn"), Res()) for _ in range(2)])
    bgrp = Ring([banks[0:4], banks[4:8]])

    def normed(row0, gain_ap, r_gain, out_ap, r_out):
        x, rx = raw.next()
        kb.dma("sp", x[:], kb.FT[row0:row0 + 128, :], (), [rx], rx)
        sq, rs = sqr.next()
        rn, rr = rnr.next()
        kb.tt("pool", sq[:], x[:], x[:], ALU.mult, [rx], [rs])
        bg = bgrp.next()
        for n in range(4):
            kb.mm(bg[n][0][:], ones[:], sq[:, n * 512:(n + 1) * 512], True, True, [r_ones, rs], [bg[n][1]])
        for n in range(4):
            kb.act(rn[:, n * 512:(n + 1) * 512], bg[n][0][:], AF.Sqrt, [bg[n][1]], [rr, bg[n][1]], bias=EPS, scale=1.0 / 128)
        kb.recip(rn[:], rn[:], [rr], [rr])
        kb.stt("dve", out_ap, x[:], gain_ap, rn[:], ALU.mult, ALU.mult, [rx, r_gain, rr, r_out], [r_out])

    for g in range(3):
        normed(R_NQ + (hk * 3 + g) * 128, qn[:, 0:1], r_qn, qT3[:, g, :], r_q3)
    normed(R_KSLC + hk * 128, kn[:, 1:2], r_kn, kslcT[:], r_ks)
    normed(R_KWIN + hk * 128, kn[:, 2:3], r_kn, kwinT[:], r_kw)
    vs = P.sb([128, 16, 129], BF16, "vs")
    vw = P.sb([128, 16, 129], BF16, "vw")
    r_vs, r_vw = Res(), Res()
    kb.dma("pool", vs[:, :, 0:128], kb.TM[:, TM_VSLC + hk * 128:TM_VSLC + (hk + 1) * 128].rearrange("(c p) d -> p c d", p=128),
           (), [r_vs], r_vs)
    kb.dma("pool", vw[:, :, 0:128], kb.TM[:, TM_VWIN + hk * 128:TM_VWIN + (hk + 1) * 128].rearrange("(c p) d -> p c d", p=128),
           (), [r_vw], r_vw)
    kb.memset("pool", vs[:, :, 128:129], 1.0, [r_vs])
    kb.memset("pool", vw[:, :, 128:129], 1.0, [r_vw])
    kcT = P.sb([128, 127], BF16, "kcT")
    r_kcT = Res()
    vc = P.sb([128, 161], BF16, "vc")
    r_vc = Res()
    kb.dma("pool", vc[0:127, 129:161], kb.c_ovl, (), [r_vc], r_vc)
    kb.memset("pool", vc[:, 128:129], 1.0, [r_vc])
    tokA = P.sb([128, SEQ], BF16, "tokA")
    tokB = P.sb([128, SEQ], BF16, "tokB")
    hidT = P.sb([128, 128], BF16, "hidT")
    kcn = P.sb([128, 128], F32, "kcn")
    junk = P.sb([128, 128], F32, "junk")
    ss1 = P.sb([128, 1], F32, "ss1")
    r_tA, r_tB, r_hid, r_kcn, r_junk, r_ss1 = Res(), Res(), Res(), Res(), Res(), Res()
    for kv in range(2):
        x, rx = raw.next()
        row0 = (R_KCMP if kv == 0 else R_VCMP) + hk * 128
        kb.dma("sp", x[:], kb.FT[row0:row0 + 128, :], (), [rx], rx)
        pe, r_pe = P.load_const(kb.cposT[l, kv], [128, 32])
        w1, r_w1 = P.load_const(kb.cmp_w1[l, kv].rearrange("(j d) e -> d j e", d=128), [128, 32, 128], BF16, q="pool")
        w2, r_w2 = P.load_const(kb.cmp_w2[l, kv], [128, 128], BF16, q="pool")
        xv = x[:].rearrange("p (n j) -> p n j", j=16)
        kb.tt("dve", tokA[:].rearrange("p (n j) -> p n j", j=16), xv, pe[:, 0:16].unsqueeze(1).to_broadcast([128, 128, 16]),
              ALU.add, [rx, r_pe], [r_tA])
        kb.tt("dve", tokB[:].rearrange("p (n j) -> p n j", j=16), xv, pe[:, 16:32].unsqueeze(1).to_broadcast([128, 128, 16]),
              ALU.add, [rx, r_pe], [r_tB])
        bk, rb = banks[0]
        for j in range(32):
            src = tokA if j < 16 else tokB
            kb.mm(bk[:, 0:127], w1[:, j, :], src[:, j:j + 16 * 126 + 1:16], j == 0, j == 31, [r_w1, r_tA, r_tB], [rb])
        kb.act(hidT[:, 0:127], bk[:, 0:127], AF.Silu, [rb], [r_hid, rb])
        bk2, rb2 = banks[1]
        kb.mm(bk2[0:127, 0:128], hidT[:, 0:127], w2[:], True, True, [r_hid, r_w2], [rb2])
        if kv == 0:
            kb.memset("pool", ss1[:], 0.0, [r_ss1])
            kb.act(junk[0:127, :], bk2[0:127, 0:128], AF.Square, [rb2, r_ss1], [r_junk, r_ss1, rb2], accum=ss1[0:127, :])
            kb.act(ss1[0:127, :], ss1[0:127, :], AF.Sqrt, [r_ss1], [r_ss1], bias=EPS, scale=1.0 / 128)
            kb.recip(ss1[0:127, :], ss1[0:127, :], [r_ss1], [r_ss1])
            kb.stt("dve", kcn[0:127, :], bk2[0:127, 0:128], ss1[0:127, 0:1], k0row[0:127, :], ALU.mult, ALU.mult,
                   [rb2, r_ss1, r_k0], [r_kcn, rb2])
            bk3, rb3 = banks[2]
            kb.tr(bk3[:, 0:127], kcn[0:127, :], ident[0:127, 0:127], [r_kcn, r_id], [rb3])
            kb.cp("act", kcT[:], bk3[:, 0:127], [rb3], [r_kcT, rb3])
        else:
            kb.cp("act", vc[0:127, 0:128], bk2[0:127, 0:128], [rb2], [r_vc, rb2])
    pTr = Ring([(P.sb([128, 384], BF16, "pT"), Res()) for _ in range(3)])
    scr = Ring(banks[0:3])
    accC, r_accC = banks[3]
    accW, r_accW = banks[4]
    accS, r_accS = banks[5]
    m1, r_m1 = banks[6]
    m2, r_m2 = banks[7]
    nmT = P.sb([32, 3, 128], BF16, "nmT")
    r_nmT = Res()
    oacc = P.sb([128, 3, 128], F32, "oacc")
    r_oacc = Res()
    ctst = P.sb([128, 3, SEQ], BF16, "ctst")
    r_ct = Res()
    imp = P.sb([128, 32], F32, "imp")
    top8 = P.sb([128, 8], F32, "top8")
    negm = P.sb([128, 32], F32, "negm")
    rden = P.sb([128, 3], F32, "rden")
    fac = P.sb([128, 3], F32, "fac")
    r_imp, r_top8, r_negm, r_rden, r_fac = Res(), Res(), Res(), Res(), Res()

    def q_rhs(i):
        return qT3[:, :, i * 128:(i + 1) * 128]

    jobs = []
    for i in range(16):
        jobs.append(("cmp", i, 0, True, True))
        wj = [c for c in range(5) if i - 4 + c >= 0]
        for c in wj:
            jobs.append(("win", i, c, c == wj[0], c == wj[-1]))
        for kc in range(i + 1):
            jobs.append(("slc", i, kc, kc == 0, kc == i))

    def stage_a(job):
        kind, i, c, first, last = job
        bk, rb = scr.next()
        if kind == "cmp":
            kb.mm(bk[0:127, 0:384].rearrange("p (g t) -> p g t", g=3), kcT[:], q_rhs(i), True, True, [r_kcT, r_q3], [rb])
        elif kind == "win":
            kc = i - 4 + c
            kb.mm(bk[:, 0:384].rearrange("p (g t) -> p g t", g=3), kwinT[:, kc * 128:(kc + 1) * 128], q_rhs(i), True, True, [r_kw, r_q3], [rb])
        else:
            o3 = bk[:, 0:384].rearrange("p (g t) -> p g t", g=3)
            kb.mm(o3, kslcT[:, c * 128:(c + 1) * 128], q_rhs(i), True, False, [r_ks, r_q3], [rb])
            kb.mm(o3, E32[:, c * 128:(c + 1) * 128], nmT[:], False, True, [r_E32, r_nmT], [rb])
        return bk, rb

    def stage_b(job, bk, rb):
        kind, i, c, first, last = job
        pT, rp = pTr.next()
        np_ = 127 if kind == "cmp" else 128
        kb.act(pT[0:np_, :], bk[0:np_, 0:384], AF.Exp, [rb], [rp, rb], scale=SC)
        p3 = pT[0:np_, :].rearrange("p (g t) -> p g t", g=3)
        if kind == "cmp":
            kb.tt("dve", p3, p3, TCb[:, :, i * 128:(i + 1) * 128], ALU.mult, [rp, r_TCb], [rp])
        elif kind == "win":
            if c == 0:
                kb.tt("dve", p3, p3, mW0[:].unsqueeze(1).to_broadcast([128, 3, 128]), ALU.mult, [rp, r_mW0], [rp])
            elif c >= 3:
                kb.tt("dve", p3, p3, TWb[:, c - 3, :, :], ALU.mult, [rp, r_TWb], [rp])
        else:
            if c == i:
                kb.tt("dve", p3, p3, TWb[:, 1, :, :], ALU.mult, [rp, r_TWb], [rp])
            elif c == i - 1:
                kb.tt("dve", p3, p3, TWb[:, 0, :, :], ALU.mult, [rp, r_TWb], [rp])
        return pT, rp

    def pv(out, lhsT, rhs, start, reads, writes):
        return kb.S.op("pe", lambda e: e.matmul(out, lhsT=lhsT, rhs=rhs, start=start, stop=True, skip_group_check=True),
                       reads, writes)

    def combine(acc, r_acc, br, i, firstbr):
        w = 161 if br == 0 else 129
        kb.ts("dve", rden[:], acc[:, 128:128 + 2 * w + 1:w], 1e-30, None, ALU.max, None, [r_acc], [r_rden, r_acc])
        kb.recip(rden[:], rden[:], [r_rden], [r_rden])
        kb.tt("dve", fac[:], rden[:], gt[:, i, br * 6 + hk * 3:br * 6 + hk * 3 + 3], ALU.mult, [r_rden, r_gt], [r_fac])
        for g in range(3):
            if firstbr:
                kb.ts("dve", oacc[:, g, :], acc[:, g * w:g * w + 128], fac[:, g:g + 1], None, ALU.mult, None,
                      [r_acc, r_fac], [r_oacc, r_acc])
            else:
                kb.stt("dve", oacc[:, g, :], acc[:, g * w:g * w + 128], fac[:, g:g + 1], oacc[:, g, :], ALU.mult, ALU.add,
                       [r_acc, r_fac, r_oacc], [r_oacc, r_acc])

    def stage_c(job, pT, rp):
        kind, i, c, first, last = job
        if kind == "cmp":
            for g in range(3):
                pv(accC[:, g * 161:(g + 1) * 161], pT[0:127, g * 128:(g + 1) * 128], vc[0:127, :], True, [rp, r_vc], [r_accC])
            combine(accC, r_accC, 0, i, True)
            for g in range(3):
                if g == 0:
                    kb.ts("dve", imp[:], accC[:, 129:161], rden[:, 0:1], None, ALU.mult, None, [r_accC, r_rden], [r_imp, r_accC])
                else:
                    kb.stt("dve", imp[:], accC[:, g * 161 + 129:g * 161 + 161], rden[:, g:g + 1], imp[:], ALU.mult, ALU.add,
                           [r_accC, r_rden, r_imp], [r_imp, r_accC])
            kb.tt("dve", imp[:], imp[:], keep[:, i, :], ALU.mult, [r_imp, r_keep], [r_imp])
            kb.tt("dve", imp[:], imp[:], addt[:, i, :], ALU.add, [r_imp, r_add], [r_imp])
            kb.S.op("dve", lambda e: e.max(out=top8[:], in_=imp[:]), [r_imp], [r_top8])
            kb.ts("dve", negm[:], imp[:], top8[:, 7:8], -30000.0, ALU.is_lt, ALU.mult, [r_imp, r_top8], [r_negm])
            kb.tr(m1[0:32, 0:128], negm[:], ident[:], [r_negm, r_id], [r_m1])
            kb.cp("act", nmT[:], m1[0:32, 0:128].unsqueeze(1).to_broadcast([32, 3, 128]), [r_m1], [r_nmT, r_m1])
            return
        acc, r_acc, vv, r_vv = (accW, r_accW, vw, r_vw) if kind == "win" else (accS, r_accS, vs, r_vs)
        kc = (i - 4 + c) if kind == "win" else c
        for g in range(3):
            pv(acc[:, g * 129:(g + 1) * 129], pT[:, g * 128:(g + 1) * 128], vv[:, kc, :], first and g == 0, [rp, r_vv], [r_acc])
        if last:
            combine(acc, r_acc, 2 if kind == "win" else 1, i, False)
            if kind == "slc":
                for g in range(3):
                    kb.tr(m2[:, g * 128:(g + 1) * 128], oacc[:, g, :], ident[:], [r_oacc, r_id], [r_m2])
                kb.cp("act", ctst[:, :, i * 128:(i + 1) * 128], m2[:, 0:384].rearrange("p (g t) -> p g t", g=3), [r_m2], [r_ct, r_m2])

    nxt = stage_a(jobs[0])
    for j, job in enumerate(jobs):
        cur = nxt
        pT, rp = stage_b(job, *cur)
        if j + 1 < len(jobs):
            nxt = stage_a(jobs[j + 1])
        stage_c(job, pT, rp)
    for g in range(3):
        h = hk * 3 + g
        kb.dma("sp", kb.CT[768 + h * 128:768 + (h + 1) * 128, :], ctst[:, g, :], [r_ct], (), r_ct)
    P.close()


def build_all(nseq=2, layers=(0, 1, 2, 3), debug=False):
    kb = KB(nseq=nseq, debug=debug)
    phase_tables(kb)
    for li, l in enumerate(layers):
        first = li == 0
        last = li == len(layers) - 1
        for s in range(nseq):
            phase_ab(kb, l, s, first)
            phase_sconv(kb, l)
            phase_g1(kb, l)
            phase_g2(kb, l)
            phase_nsa(kb, l, 0)
            phase_nsa(kb, l, 1)
            for nb in range(4):
                phase_cd(kb, l, s, nb, first, last)
    return kb


_CACHE = {}


def kernel(**inputs):
    x = np.asarray(inputs["x"], dtype=np.float32)
    B = x.shape[0]
    nseq = B // NCORES
    sh = shared_inputs(inputs)
    if "kb" not in _CACHE:
        _CACHE["kb"] = build_all(nseq=nseq)
    kb = _CACHE["kb"]
    in_maps = []
    for c in range(NCORES):
        m = dict(sh)
        m["xT"] = np.ascontiguousarray(x[c * nseq:(c + 1) * nseq].reshape(nseq * SEQ, D).T)
        in_maps.append(m)
    res = run_bass_kernel_spmd(kb.nc, in_maps, core_ids=list(range(NCORES)))
    out = np.empty((B, SEQ, D), np.float32)
    for c in range(NCORES):
        out[c * nseq:(c + 1) * nseq] = np.asarray(res.results[c]["outT"]).T.reshape(nseq, SEQ, D)
    return out
```

```python
import math
import numpy as np
import concourse.bass as bass
import concourse.mybir as mybir
from concourse.bass_utils import run_bass_kernel_spmd
from contextlib import ExitStack

F32 = mybir.dt.float32
BF16 = mybir.dt.bfloat16
AF = mybir.ActivationFunctionType
ALU = mybir.AluOpType

D = 2048
SEQ = 2048
DEPTH = 4
PROJ = 6942
DFF = 5632
NH = 6
EPS = 1e-6
NCORES = 8
import os
STORE_Q = os.environ.get('STORE_Q', 'pool')
NOCAST1 = os.environ.get('NOCAST1', '') == '1'

ENGS = ("pe", "act", "dve", "pool", "sp")
SIG_CAP = 30000


class Res:
    __slots__ = ("writer", "readers", "dsem", "last_dma")

    def __init__(self):
        self.writer = None
        self.readers = {}
        self.dsem = None
        self.last_dma = None


class Op:
    __slots__ = ("eng", "fn", "deps", "needs_sig", "sem", "val", "dma", "done")

    def __init__(self, eng, fn):
        self.eng = eng
        self.fn = fn
        self.deps = ()
        self.needs_sig = False
        self.sem = None
        self.val = 0
        self.dma = False
        self.done = False


class Sched:
    def __init__(self, nc, stack, n_eng_epochs=8, n_dma_sems=62):
        self.nc = nc
        self.eng_sems = {e: [stack.enter_context(nc.semaphore(f"s_{e}{i}")) for i in range(n_eng_epochs)]
                         for e in ENGS if e != "sp"}
        self.dma_sems = [stack.enter_context(nc.semaphore(f"s_dma{i}")) for i in range(n_dma_sems)]
        self.dma_cnt = [0] * n_dma_sems
        self.dma_free = list(range(n_dma_sems))
        self.sig_cnt = {e: 0 for e in ENGS}
        self.waited = {e: {} for e in ENGS}
        self.pending = {e: [] for e in ENGS}
        self.phase_dma_res = []
        self.n_ops = 0

    def op(self, eng, fn, reads=(), writes=(), dma=None):
        o = Op(eng, fn)
        self.n_ops += 1
        deps = {}
        for r in reads:
            if r.writer is not None:
                deps[id(r.writer)] = r.writer
        for w in writes:
            if w.writer is not None:
                deps[id(w.writer)] = w.writer
            for rd in w.readers.values():
                deps[id(rd)] = rd
        if dma is not None:
            o.dma = True
            if dma.dsem is None:
                assert self.dma_free, "out of DMA semaphores in this phase"
                self.dma_free.sort(key=lambda i: self.dma_cnt[i])
                dma.dsem = self.dma_free.pop(0)
                self.phase_dma_res.append(dma)
            if dma.last_dma is not None:
                deps[id(dma.last_dma)] = dma.last_dma
            self.dma_cnt[dma.dsem] += 16
            o.sem = self.dma_sems[dma.dsem]
            o.val = self.dma_cnt[dma.dsem]
            dma.last_dma = o
        dl = []
        for d in deps.values():
            if d is o or d.done:
                continue
            if (not d.dma) and d.eng == "pe" and eng == "pe" and not o.dma:
                continue
            if not d.dma:
                d.needs_sig = True
            dl.append(d)
        o.deps = dl
        for r in reads:
            key = ("dma", id(o)) if o.dma else eng
            r.readers[key] = o
        for w in writes:
            w.writer = o
            w.readers = {}
        self.pending[eng].append(o)
        return o

    def barrier(self):
        tails = []
        for e in ENGS:
            for o in reversed(self.pending[e]):
                if not o.dma and o.fn is not None:
                    tails.append(o)
                    break
        dmas = [r.last_dma for r in self.phase_dma_res if r.last_dma is not None]
        for e in ENGS:
            o = Op(e, None)
            dl = []
            for t in tails:
                if t.eng != e:
                    t.needs_sig = True
                    dl.append(t)
            dl.extend(dmas)
            o.deps = dl
            self.pending[e].append(o)

    def _assign(self):
        for e in ENGS:
            for o in self.pending[e]:
                if o.dma or o.fn is None:
                    continue
                if o.needs_sig:
                    self.sig_cnt[e] += 1
                    c = self.sig_cnt[e]
                    ep = (c - 1) // SIG_CAP
                    o.sem = self.eng_sems[e][ep]
                    o.val = c - ep * SIG_CAP

    def flush(self):
        nc = self.nc
        self._assign()
        pend = self.pending
        sched = self

        def emit(e, engobj):
            waited = sched.waited[e]
            for o in pend[e]:
                for d in o.deps:
                    key = id(d.sem)
                    if waited.get(key, 0) >= d.val:
                        continue
                    engobj.wait_ge(d.sem, d.val)
                    waited[key] = d.val
                if o.fn is None:
                    continue
                ins = o.fn(engobj)
                if o.dma:
                    ins.then_inc(o.sem, 16)
                elif o.needs_sig:
                    ins.then_inc(o.sem, 1)

        with nc.Block() as block:
            @block.tensor
            def _(eng):
                emit("pe", eng)

            @block.scalar
            def _(eng):
                emit("act", eng)

            @block.vector
            def _(eng):
                emit("dve", eng)

            @block.gpsimd
            def _(eng):
                emit("pool", eng)

            @block.sync
            def _(eng):
                emit("sp", eng)

        for e in ENGS:
            for o in self.pending[e]:
                o.done = True
        self.pending = {e: [] for e in ENGS}
        for r in self.phase_dma_res:
            self.dma_free.append(r.dsem)
            r.dsem = None
            r.last_dma = None
        self.phase_dma_res = []

    def end_phase(self):
        self.barrier()
        self.flush()


class Ring:
    def __init__(self, items):
        self.items = items
        self.i = 0

    def next(self):
        it = self.items[self.i % len(self.items)]
        self.i += 1
        return it


R_GQ, R_GK, R_GV, R_NQ, R_KCMP, R_VCMP, R_KSLC, R_KWIN, R_CU, R_CB, R_CC = (
    0, 768, 1536, 2304, 3072, 3328, 3584, 3840, 4096, 4608, 5120)
FT_ROWS = 5632
FT_SEGS = [(0, 2304, 0), (3084, 768, 2304), (3852, 768, 3072), (4876, 256, 3840), (5406, 1536, 4096)]
TM_W = 1310
TM_SEGS = [(2304, 780, 0), (4620, 256, 780), (5132, 274, 1036)]
TM_Z, TM_B, TM_A, TM_VSLC, TM_VWIN, TM_GATE = 0, 768, 774, 780, 1036, 1292


def _blocks(segs, maxw=512):
    out = []
    for c0, n, r0 in segs:
        o = 0
        while o < n:
            w = min(maxw, n - o)
            out.append((c0 + o, w, r0 + o))
            o += w
    return out


FT_BLOCKS = _blocks(FT_SEGS)
TM_BLOCKS = _blocks(TM_SEGS)


class KB:
    def __init__(self, nseq=2, layers=(0, 1, 2, 3), debug=False, parts=("ab", "sconv", "gdn", "nsa", "cd"), dbg_in=()):
        self.nseq = nseq
        self.NT = nseq * SEQ
        self.layers = layers
        self.debug = debug
        self.parts = parts
        self.uid = 0
        nc = self.nc = bass.Bass("TRN2", target_bir_lowering=False)
        NT = self.NT

        def din(name, shape, dt=F32):
            return nc.dram_tensor(name, list(shape), dt, kind="ExternalInput").ap()

        def dscr(name, shape, dt=F32):
            kind = "ExternalOutput" if debug else "Internal"
            if name in dbg_in:
                kind = "ExternalInput"
            return nc.dram_tensor(name, list(shape), dt, kind=kind).ap()

        self.xT = din("xT", [D, NT])
        self.w_in = din("w_in", [DEPTH, D, PROJ])
        self.w_out = din("w_out", [DEPTH, D, D])
        self.w_gate = din("w_gate", [DEPTH, D, DFF])
        self.w_up = din("w_up", [DEPTH, D, DFF])
        self.w_down = din("w_down", [DEPTH, DFF, D])
        self.cmp_w1 = din("cmp_w1", [DEPTH, 2, 4096, 128])
        self.cmp_w2 = din("cmp_w2", [DEPTH, 2, 128, 128])
        self.nmixT = din("nmixT", [DEPTH, 128, 16])
        self.nffnT = din("nffnT", [DEPTH, 128, 16])
        self.gconvT = din("gconvT", [DEPTH, 128, 18, 4])
        self.galog = din("galog", [DEPTH, 128, 6])
        self.gdtb = din("gdtb", [DEPTH, 128, 6])
        self.gnorm_row = din("gnorm_row", [DEPTH, 128, 128])
        self.qnormT = din("qnormT", [DEPTH, 128, 1])
        self.knormT = din("knormT", [DEPTH, 128, 3])
        self.knorm0_row = din("knorm0_row", [DEPTH, 128, 128])
        self.cposT = din("cposT", [DEPTH, 2, 128, 32])
        self.sconvT = din("sconvT", [DEPTH, 128, 4, 3])
        self.relb_rep = din("relb_rep", [128, 192])
        self.c_idxC = din("c_idxC", [127, 2048])
        self.c_idxW = din("c_idxW", [128, 2, 128])
        self.c_maskW0 = din("c_maskW0", [128, 128])
        self.c_keep = din("c_keep", [128, 16, 32])
        self.c_add = din("c_add", [128, 16, 32])
        self.c_E32 = din("c_E32", [32, 2048])
        self.c_ovl = din("c_ovl", [127, 32])
        self.c_triu = din("c_triu", [128, 128])
        self.c_msl = din("c_msl", [128, 128])
        self.c_mui = din("c_mui", [128, 128])
        self.c_ident = din("c_ident", [128, 128])

        self.outT = nc.dram_tensor("outT", [D, NT], F32, kind="ExternalOutput").ap()
        self.XM = dscr("XM", [D, NT])
        self.FT = dscr("FT", [FT_ROWS, SEQ])
        self.TM = dscr("TM", [SEQ, TM_W])
        self.GQ = dscr("GQ", [2304, SEQ])
        self.CT = dscr("CT", [D, SEQ], BF16)
        self.TCd = dscr("TCd", [127, 6 * 2048], BF16)
        self.TWd = dscr("TWd", [128, 2 * 6 * 128], BF16)

        self.WBo = dscr("WBo", [DEPTH, 4, 128, 16, 512], BF16)
        self.WBg = dscr("WBg", [DEPTH, 11, 128, 16, 512], BF16)
        self.WBu = dscr("WBu", [DEPTH, 11, 128, 16, 512], BF16)
        self.WBd = dscr("WBd", [DEPTH, 8, 128, 44, 256], BF16)

        self.top = ExitStack()
        self.S = Sched(nc, self.top)

    def name(self, p):
        self.uid += 1
        return f"{p}{self.uid}"

    def mm(self, out, lhsT, rhs, start, stop, reads, writes):
        return self.S.op("pe", lambda e: e.matmul(out, lhsT=lhsT, rhs=rhs, start=start, stop=stop), reads, writes)

    def tr(self, out, in_, ident, reads, writes):
        return self.S.op("pe", lambda e: e.transpose(out, in_, ident), reads, writes)

    def act(self, out, in_, func, reads, writes, bias=None, scale=None, accum=None):
        kw = {}
        if bias is not None:
            kw["bias"] = bias
        if scale is not None:
            kw["scale"] = scale
        if accum is not None:
            kw["accum_out"] = accum
        return self.S.op("act", lambda e: e.activation(out=out, in_=in_, func=func, **kw), reads, writes)

    def amul(self, out, in_, mul, reads, writes):
        return self.S.op("act", lambda e: e.mul(out=out, in_=in_, mul=mul), reads, writes)

    def tt(self, eng, out, in0, in1, op, reads, writes):
        return self.S.op(eng, lambda e: e.tensor_tensor(out=out, in0=in0, in1=in1, op=op), reads, writes)

    def stt(self, eng, out, in0, scalar, in1, op0, op1, reads, writes):
        return self.S.op(eng, lambda e: e.scalar_tensor_tensor(out=out, in0=in0, scalar=scalar, in1=in1, op0=op0, op1=op1),
                         reads, writes)

    def ts(self, eng, out, in0, s1, s2, op0, op1, reads, writes):
        if s2 is None:
            return self.S.op(eng, lambda e: e.tensor_scalar(out=out, in0=in0, scalar1=s1, scalar2=None, op0=op0), reads, writes)
        return self.S.op(eng, lambda e: e.tensor_scalar(out=out, in0=in0, scalar1=s1, scalar2=s2, op0=op0, op1=op1), reads, writes)

    def cp(self, eng, out, in_, reads, writes):
        if eng == "act":
            return self.S.op("act", lambda e: e.copy(out=out, in_=in_), reads, writes)
        return self.S.op(eng, lambda e: e.tensor_copy(out=out, in_=in_), reads, writes)

    def recip(self, out, in_, reads, writes):
        return self.S.op("dve", lambda e: e.reciprocal(out=out, in_=in_), reads, writes)

    def memset(self, eng, ap, val, writes):
        return self.S.op(eng, lambda e: e.memset(ap, val), (), writes)

    def dma(self, q, out, in_, reads, writes, res):
        return self.S.op(q, lambda e: e.dma_start(out=out, in_=in_), reads, writes, dma=res)


class Phase:
    def __init__(self, kb):
        self.kb = kb
        self.nc = kb.nc
        self.st = ExitStack()

    def sb(self, shape, dt, tag="t"):
        return self.st.enter_context(self.nc.sbuf_tensor(self.kb.name(tag), list(shape), dt))

    def ps(self, shape, dt=F32, tag="p"):
        return self.st.enter_context(self.nc.psum_tensor(self.kb.name(tag), list(shape), dt))

    def banks(self, n=8):
        return [(self.ps([128, 512]), Res()) for _ in range(n)]

    def load_const(self, dram_ap, shape, dt=F32, q="sp"):
        t = self.sb(shape, dt, "c")
        r = Res()
        self.kb.dma(q, t[:], dram_ap, (), [r], r)
        return t, r

    def close(self):
        self.kb.S.end_phase()
        self.st.close()


def phase_ab(kb, l, s, first):
    P = Phase(kb)
    S = kb.S
    xsrc = kb.xT if first else kb.XM
    t0 = s * SEQ
    hT = P.sb([128, 16, SEQ], BF16, "hT")
    r_h = [Res() for _ in range(16)]
    ones = P.sb([128, 128], BF16, "ones")
    r_ones = Res()
    kb.memset("pool", ones[:], 1.0, [r_ones])
    g1, r_g1 = P.load_const(kb.nmixT[l], [128, 16])
    xk = Ring([(P.sb([128, SEQ], F32, "xk"), Res()) for _ in range(2)])
    sq = Ring([(P.sb([128, SEQ], BF16, "sq"), Res()) for _ in range(2)])
    rstd = P.sb([128, SEQ], F32, "rstd")
    r_rstd = [Res() for _ in range(4)]
    banks = P.banks(8)
    for k in range(16):
        xt, rx = xk.next()
        kb.dma("sp", xt[:], xsrc[k * 128:(k + 1) * 128, t0:t0 + SEQ], (), [rx], rx)
        st, rs = sq.next()
        kb.act(st[:], xt[:], AF.Square, [rx], [rs])
        for n in range(4):
            kb.mm(banks[n][0][:], ones[:], st[:, n * 512:(n + 1) * 512], k == 0, k == 15, [r_ones, rs], [banks[n][1]])
    for n in range(4):
        kb.act(rstd[:, n * 512:(n + 1) * 512], banks[n][0][:], AF.Sqrt, [banks[n][1]], [r_rstd[n]], bias=EPS, scale=1.0 / D)
        kb.recip(rstd[:, n * 512:(n + 1) * 512], rstd[:, n * 512:(n + 1) * 512], [r_rstd[n]], [r_rstd[n]])
    for k in range(16):
        xt, rx = xk.next()
        kb.dma("sp", xt[:], xsrc[k * 128:(k + 1) * 128, t0:t0 + SEQ], (), [rx], rx)
        kb.stt("dve", hT[:, k, :], xt[:], g1[:, k:k + 1], rstd[:], ALU.mult, ALU.mult, [rx, r_g1] + r_rstd, [r_h[k]])
    wring = Ring([(P.sb([128, 16, 512], BF16, "wt"), Res()) for _ in range(2)])
    stage = Ring([(P.sb([128, SEQ], F32, "stg"), [Res() for _ in range(4)], Res()) for _ in range(2)])
    bring = Ring(banks)
    ev = 0
    for (c0, ncol, r0) in FT_BLOCKS:
        wt, rw = wring.next()
        kb.dma("pool", wt[:, :, 0:ncol], kb.w_in[l, :, c0:c0 + ncol].rearrange("(k p) c -> p k c", p=128), (), [rw], rw)
        for m in range(ncol // 128):
            stg, rstg, rdma = stage.next()
            for n in range(4):
                bk, rb = bring.next()
                for k in range(16):
                    kb.mm(bk[:], wt[:, k, m * 128:(m + 1) * 128], hT[:, k, n * 512:(n + 1) * 512], k == 0, k == 15,
                          [rw, r_h[k]], [rb])
                eng = "act" if ev % 2 == 0 else "dve"
                ev += 1
                kb.cp(eng, stg[:, n * 512:(n + 1) * 512], bk[:], [rb], [rstg[n]])
            kb.dma("sp", kb.FT[r0 + m * 128:r0 + (m + 1) * 128, :], stg[:], rstg, (), rdma)
    stage2 = Ring([(P.sb([128, 512], F32, "stg2"), Res()) for _ in range(3)])
    for (c0, ncol, tc0) in TM_BLOCKS:
        wt, rw = wring.next()
        kb.dma("pool", wt[:, :, 0:ncol], kb.w_in[l, :, c0:c0 + ncol].rearrange("(k p) c -> p k c", p=128), (), [rw], rw)
        for t in range(16):
            bk, rb = bring.next()
            for k in range(16):
                kb.mm(bk[:, 0:ncol], hT[:, k, t * 128:(t + 1) * 128], wt[:, k, 0:ncol], k == 0, k == 15, [rw, r_h[k]], [rb])
            stg, rs = stage2.next()
            eng = "act" if ev % 2 == 0 else "dve"
            ev += 1
            kb.cp(eng, stg[:, 0:ncol], bk[:, 0:ncol], [rb], [rs])
            kb.dma("sp", kb.TM[t * 128:(t + 1) * 128, tc0:tc0 + ncol], stg[:, 0:ncol], [rs], (), rs)
    P.close()


def phase_sconv(kb, l):
    P = Phase(kb)
    w, rw = P.load_const(kb.sconvT[l], [128, 4, 3])
    ring = Ring([[(P.sb([128, SEQ], F32, "sc"), Res()) for _ in range(3)] for _ in range(2)])
    vbuf = Ring([(P.sb([128, SEQ], F32, "scv"), Res()) for _ in range(2)])
    ybuf = Ring([(P.sb([128, SEQ], F32, "scy"), Res()) for _ in range(2)])
    obuf = Ring([(P.sb([128, SEQ], BF16, "sco"), Res()) for _ in range(2)])
    for g in range(4):
        (cu, rcu), (cb, rcb), (cc, rcc) = ring.next()
        kb.dma("sp", cu[:], kb.FT[R_CU + g * 128:R_CU + (g + 1) * 128, :], (), [rcu], rcu)
        kb.dma("sp", cb[:], kb.FT[R_CB + g * 128:R_CB + (g + 1) * 128, :], (), [rcb], rcb)
        kb.dma("sp", cc[:], kb.FT[R_CC + g * 128:R_CC + (g + 1) * 128, :], (), [rcc], rcc)
        v, rv = vbuf.next()
        y, ry = ybuf.next()
        o, ro = obuf.next()
        kb.tt("pool", v[:], cc[:], cu[:], ALU.mult, [rcc, rcu], [rv])
        kb.ts("dve", y[:], v[:], w[:, g, 2:3], None, ALU.mult, None, [rv, rw], [ry])
        kb.stt("dve", y[:, 1:SEQ], v[:, 0:SEQ - 1], w[:, g, 1:2], y[:, 1:SEQ], ALU.mult, ALU.add, [rv, rw, ry], [ry])
        kb.stt("dve", y[:, 2:SEQ], v[:, 0:SEQ - 2], w[:, g, 0:1], y[:, 2:SEQ], ALU.mult, ALU.add, [rv, rw, ry], [ry])
        kb.tt("pool", o[:], cb[:], y[:], ALU.mult, [rcb, ry], [ro])
        kb.dma("sp", kb.CT[1536 + g * 128:1536 + (g + 1) * 128, :], o[:], [ro], (), ro)
    P.close()


def phase_cd(kb, l, s, nb, first, last):
    P = Phase(kb)
    xsrc = kb.xT if first else kb.XM
    xdst = kb.outT if last else kb.XM
    t0 = s * SEQ + nb * 512
    c0 = nb * 512
    act = P.sb([128, 44, 512], BF16, "act")
    r_act = [Res() for _ in range(44)]
    xmid = P.sb([128, 16, 512], F32, "xmid")
    r_x = [Res() for _ in range(16)]
    h2 = P.sb([128, 16, 512], BF16, "h2")
    r_h2 = [Res() for _ in range(16)]
    ones = P.sb([128, 128], BF16, "ones")
    r_ones = Res()
    kb.memset("pool", ones[:], 1.0, [r_ones])
    g2, r_g2 = P.load_const(kb.nffnT[l], [128, 16])
    rstd = P.sb([128, 512], F32, "rstd")
    r_rstd = Res()
    sqr = Ring([(P.sb([128, 512], BF16, "sq"), Res()) for _ in range(2)])
    wring = Ring([(P.sb([128, 16, 512], BF16, "wt"), Res()) for _ in range(3)])
    wdring = Ring([(P.sb([128, 44, 256], BF16, "wd"), Res()) for _ in range(2)])
    banks = P.banks(8)
    bring = Ring(banks[0:7])
    ssb, r_ssb = banks[7]
    wts = []
    for mb in range(4):
        wt, rw = wring.next()
        wts.append((wt, rw))
        if mb < 3:
            kb.dma("sp", wt[:], kb.WBo[l, mb], (), [rw], rw)
        if mb == 0:
            for k in range(16):
                kb.dma("sp", act[:, k, :], kb.CT[k * 128:(k + 1) * 128, c0:c0 + 512], (), [r_act[k]], r_act[k])
            for m in range(16):
                kb.dma("sp", xmid[:, m, :], xsrc[m * 128:(m + 1) * 128, t0:t0 + 512], (), [r_x[m]], r_x[m])
    pend = None
    for mb in range(4):
        wt, rw = wts[mb]
        if mb == 3:
            kb.dma("sp", wt[:], kb.WBo[l, mb], (), [rw], rw)
        for mi in range(4):
            m = mb * 4 + mi
            bk, rb = bring.next()
            for k in range(16):
                kb.mm(bk[:], wt[:, k, mi * 128:(mi + 1) * 128], act[:, k, :], k == 0, k == 15, [rw, r_act[k]], [rb])
            if pend is not None:
                kb.mm(ssb[:], ones[:], pend[0][:], pend[2] == 0, False, [r_ones, pend[1]], [r_ssb])
            kb.tt("dve", xmid[:, m, :], xmid[:, m, :], bk[:], ALU.add, [r_x[m], rb], [r_x[m], rb])
            sq, rs = sqr.next()
            kb.act(sq[:], xmid[:, m, :], AF.Square, [r_x[m]], [rs])
            pend = (sq, rs, m)
    kb.mm(ssb[:], ones[:], pend[0][:], False, True, [r_ones, pend[1]], [r_ssb])
    kb.act(rstd[:], ssb[:], AF.Sqrt, [r_ssb], [r_rstd], bias=EPS, scale=1.0 / D)
    kb.recip(rstd[:], rstd[:], [r_rstd], [r_rstd])
    for m in range(16):
        kb.stt("dve", h2[:, m, :], xmid[:, m, :], g2[:, m:m + 1], rstd[:], ALU.mult, ALU.mult, [r_x[m], r_g2, r_rstd], [r_h2[m]])
    sgs = [(P.sb([128, 512], BF16, "sgb"), Res()) for _ in range(8)]
    for fb in range(11):
        wg, rwg = wring.next()
        kb.dma("sp", wg[:], kb.WBg[l, fb], (), [rwg], rwg)
        wu, rwu = wring.next()
        kb.dma("sp", wu[:], kb.WBu[l, fb], (), [rwu], rwu)
        for fi in range(4):
            bg, rbg = bring.next()
            for k in range(16):
                kb.mm(bg[:], wg[:, k, fi * 128:(fi + 1) * 128], h2[:, k, :], k == 0, k == 15, [rwg, r_h2[k]], [rbg])
            sg, rsg = sgs[(fb % 2) * 4 + fi]
            kb.act(sg[:], bg[:], AF.Silu, [rbg], [rsg, rbg])
        for fi in range(4):
            f = fb * 4 + fi
            bu, rbu = bring.next()
            for k in range(16):
                kb.mm(bu[:], wu[:, k, fi * 128:(fi + 1) * 128], h2[:, k, :], k == 0, k == 15, [rwu, r_h2[k]], [rbu])
            sg, rsg = sgs[(fb % 2) * 4 + fi]
            kb.tt("dve", act[:, f, :], sg[:], bu[:], ALU.mult, [rsg, rbu], [r_act[f], rbu])
    for mb in range(8):
        wd, rwd = wdring.next()
        kb.dma("sp", wd[:], kb.WBd[l, mb], (), [rwd], rwd)
        for mi in range(2):
            m = mb * 2 + mi
            bk, rb = bring.next()
            for f in range(44):
                kb.mm(bk[:], wd[:, f, mi * 128:(mi + 1) * 128], act[:, f, :], f == 0, f == 43, [rwd, r_act[f]], [rb])
            kb.tt("dve", xmid[:, m, :], xmid[:, m, :], bk[:], ALU.add, [r_x[m], rb], [r_x[m]])
            kb.dma(STORE_Q, xdst[m * 128:(m + 1) * 128, t0:t0 + 512], xmid[:, m, :], [r_x[m]], (), r_x[m])
    P.close()


def cast_weights(kb, l, which, pool=None):
    plan = {"o": (kb.w_out[l], kb.WBo[l], 4, 512), "g": (kb.w_gate[l], kb.WBg[l], 11, 512),
            "u": (kb.w_up[l], kb.WBu[l], 11, 512), "d": (kb.w_down[l], kb.WBd[l], 8, 256)}
    src, dst, nt, w = plan[which]
    for t in range(nt):
        r = pool.next() if pool is not None else Res()
        kb.dma("pool", dst[t], src[:, t * w:(t + 1) * w].rearrange("(k p) c -> p k c", p=128), (), [r], r)


def zero_ct_rows(kb, r0, r1):
    P = Phase(kb)
    z = P.sb([128, SEQ], BF16, "z")
    rz = Res()
    kb.memset("pool", z[:], 0.0, [rz])
    for r in range(r0, r1, 128):
        kb.dma("sp", kb.CT[r:r + 128, :], z[:], [rz], (), rz)
    P.close()


def _t5_bucket_np(dist):
    n = np.maximum(dist, 0)
    nf = np.maximum(n, 1).astype(np.float32)
    large = 16 + (np.log(nf / np.float32(16)) / np.float32(math.log(8.0)) * np.float32(16)).astype(np.int32)
    large = np.minimum(large, 31)
    return np.where(n < 16, n, large)


def position_constants():
    c = {}
    t = np.arange(2048)
    n = np.arange(127)
    dist = t[None, :] - (16 * n[:, None] + 31)
    c["c_idxC"] = np.where(dist >= 0, _t5_bucket_np(dist), -1).astype(np.float32)
    kl = np.arange(128)[:, None]
    ql = np.arange(128)[None, :]
    d3 = 128 + ql - kl
    d4 = ql - kl
    idxW = np.stack([_t5_bucket_np(d3), np.where(d4 >= 0, _t5_bucket_np(d4), -1)], axis=1)
    c["c_idxW"] = idxW.astype(np.float32)
    c["c_maskW0"] = (ql < kl).astype(np.float32)
    i = np.arange(16)[None, :, None]
    q = np.arange(128)[:, None, None]
    j = np.arange(32)[None, None, :]
    cur = (128 * i + q) // 64
    future = j > cur
    forced = (j == 0) | (j == cur) | (j == cur - 1)
    c["c_keep"] = (~future & ~forced).astype(np.float32)
    c["c_add"] = np.where(future, -1.0, np.where(forced, 1e4, 0.0)).astype(np.float32)
    c["c_E32"] = (np.arange(2048)[None, :] // 64 == np.arange(32)[:, None]).astype(np.float32)
    cs = 16 * np.arange(127)[:, None]
    ss = 64 * np.arange(32)[None, :]
    c["c_ovl"] = ((cs < ss + 64) & (cs + 32 > ss)).astype(np.float32)
    p = np.arange(128)[:, None]
    f = np.arange(128)[None, :]
    c["c_triu"] = (p <= f).astype(np.float32)
    c["c_msl"] = np.where(p > f, 0.0, -1e9).astype(np.float32)
    c["c_mui"] = np.where(f >= p, 0.0, 1e9).astype(np.float32)
    c["c_ident"] = np.eye(128, dtype=np.float32)
    return c


def shared_inputs(inp):
    f = lambda a: np.ascontiguousarray(np.asarray(a, dtype=np.float32))
    sh = {}
    for k in ("w_in", "w_out", "w_gate", "w_up", "w_down", "cmp_w1", "cmp_w2"):
        sh[k] = f(inp[k])
    sh["nmixT"] = f(np.asarray(inp["norm_mix"]).reshape(DEPTH, 16, 128).transpose(0, 2, 1))
    sh["nffnT"] = f(np.asarray(inp["norm_ffn"]).reshape(DEPTH, 16, 128).transpose(0, 2, 1))
    sh["gconvT"] = f(np.asarray(inp["gdn_conv"]).reshape(DEPTH, 18, 128, 4).transpose(0, 2, 1, 3))
    sh["galog"] = f(np.broadcast_to(np.asarray(inp["gdn_a_log"])[:, None, :], (DEPTH, 128, 6)))
    sh["gdtb"] = f(np.broadcast_to(np.asarray(inp["gdn_dt_bias"])[:, None, :], (DEPTH, 128, 6)))
    sh["gnorm_row"] = f(np.broadcast_to(np.asarray(inp["gdn_norm"])[:, None, :], (DEPTH, 128, 128)))
    sh["qnormT"] = f(np.asarray(inp["nsa_q_norm"]).reshape(DEPTH, 128, 1))
    sh["knormT"] = f(np.asarray(inp["nsa_k_norm"]).transpose(0, 2, 1))
    sh["knorm0_row"] = f(np.broadcast_to(np.asarray(inp["nsa_k_norm"])[:, 0][:, None, :], (DEPTH, 128, 128)))
    sh["cposT"] = f(np.asarray(inp["cmp_pos"]).transpose(0, 1, 3, 2))
    sh["sconvT"] = f(np.asarray(inp["sconv_w"]).reshape(DEPTH, 4, 128, 3).transpose(0, 2, 1, 3))
    sh["relb_rep"] = f(np.broadcast_to(np.asarray(inp["rel_bias"]).reshape(1, 192), (128, 192)))
    sh.update(position_constants())
    return sh


def phase_g1(kb, l, casts=()):
    P = Phase(kb)
    for w_ in casts:
        cast_weights(kb, l, w_)
    cw, r_cw = P.load_const(kb.gconvT[l], [128, 18, 4])
    ones = P.sb([128, 128], F32, "ones")
    r_ones = Res()
    kb.memset("pool", ones[:], 1.0, [r_ones])
    raw = Ring([(P.sb([128, SEQ], F32, "raw"), Res()) for _ in range(2)])
    cbr = Ring([(P.sb([128, SEQ], F32, "cb"), Res()) for _ in range(3)])
    sqr = Ring([(P.sb([128, SEQ], F32, "sq"), Res()) for _ in range(2)])
    rnr = Ring([(P.sb([128, SEQ], F32, "rn"), Res()) for _ in range(2)])
    banks = P.banks(8)
    bgrp = Ring([banks[0:4], banks[4:8]])
    def chunk_gen(c):
        x, rx = raw.next()
        cb, rc = cbr.next()
        if c < 12:
            sq, rs = sqr.next()
            rn, rr = rnr.next()
            bg = bgrp.next()
        kb.dma("sp", x[:], kb.FT[c * 128:(c + 1) * 128, :], (), [rx], rx)
        yield
        kb.ts("dve", cb[:], x[:], cw[:, c, 3:4], None, ALU.mult, None, [rx, r_cw], [rc])
        yield
        for j in (1, 2, 3):
            kb.stt("dve", cb[:, j:SEQ], x[:, 0:SEQ - j], cw[:, c, 3 - j:4 - j], cb[:, j:SEQ], ALU.mult, ALU.add,
                   [rx, r_cw, rc], [rc])
            yield
        kb.act(cb[:], cb[:], AF.Silu, [rc], [rc])
        yield
        if c < 12:
            kb.tt("pool", sq[:], cb[:], cb[:], ALU.mult, [rc], [rs])
            yield
            for n in range(4):
                kb.mm(bg[n][0][:], ones[:], sq[:, n * 512:(n + 1) * 512], True, True, [r_ones, rs], [bg[n][1]])
            yield
            for n in range(4):
                kb.act(rn[:, n * 512:(n + 1) * 512], bg[n][0][:], AF.Sqrt, [bg[n][1]], [rr, bg[n][1]], bias=EPS, scale=1.0)
            yield
            kb.recip(rn[:], rn[:], [rr], [rr])
            yield
            if c < 6:
                kb.stt("dve", cb[:], cb[:], 128.0 ** -0.5, rn[:], ALU.mult, ALU.mult, [rc, rr], [rc])
            else:
                kb.tt("pool", cb[:], cb[:], rn[:], ALU.mult, [rc, rr], [rc])
            yield
        kb.dma(STORE_Q, kb.GQ[c * 128:(c + 1) * 128, :], cb[:], [rc], (), rc)
        yield

    G = 2
    for c0 in range(0, 18, G):
        _lockstep([chunk_gen(c) for c in range(c0, min(18, c0 + G))])
    P.close()


def _lockstep(gens, maxrounds=None):
    gens = list(gens)
    rounds = 0
    while gens:
        if maxrounds is not None and rounds >= maxrounds:
            break
        rounds += 1
        nxt = []
        for g in gens:
            try:
                next(g)
                nxt.append(g)
            except StopIteration:
                pass
        gens = nxt


def phase_g2(kb, l, stop=None, maxrounds=None, casts=()):
    P = Phase(kb)
    for w_ in casts:
        cast_weights(kb, l, w_)
    ident, r_id = P.load_const(kb.c_ident, [128, 128])
    triu, r_triu = P.load_const(kb.c_triu, [128, 128])
    msl, r_msl = P.load_const(kb.c_msl, [128, 128])
    mui, r_mui = P.load_const(kb.c_mui, [128, 128])
    grow, r_grow = P.load_const(kb.gnorm_row[l], [128, 128])
    alog, r_alog = P.load_const(kb.galog[l], [128, 6])
    dtb, r_dtb = P.load_const(kb.gdtb[l], [128, 6])
    ones = P.sb([128, 128], F32, "ones")
    negones = P.sb([128, 128], F32, "nones")
    r_ones, r_nones = Res(), Res()
    kb.memset("pool", ones[:], 1.0, [r_ones])
    kb.memset("pool", negones[:], -1.0, [r_nones])
    banks = P.banks(8)
    slots = [Ring([(banks[h][0][:, q * 128:(q + 1) * 128], banks[h][1]) for q in range(4)]) for h in range(6)]
    ba = P.sb([128, 16, 12], F32, "ba")
    r_ba = Res()
    kb.dma("sp", ba[:], kb.TM[:, TM_B:TM_B + 12].rearrange("(t p) c -> p t c", p=128), (), [r_ba], r_ba)

    def small(tag):
        return P.sb([128, 16, 6], F32, tag), Res()
    beta, r_beta = small("beta")
    nbeta, r_nbeta = small("nbeta")
    gg, r_g = small("g")
    gc, r_gc = small("gc")
    gl, r_gl = small("gl")
    bgs, r_bg = small("bg")
    kds, r_kds = small("kds")
    egl, r_egl = small("egl")
    nal = P.sb([128, 6], F32, "nal")
    r_nal = Res()
    kb.act(beta[:], ba[:, :, 0:6], AF.Sigmoid, [r_ba], [r_beta])
    kb.ts("dve", nbeta[:], beta[:], -1.0, None, ALU.mult, None, [r_beta], [r_nbeta])
    kb.tt("dve", gg[:], ba[:, :, 6:12], dtb[:].unsqueeze(1).to_broadcast([128, 16, 6]), ALU.add, [r_ba, r_dtb], [r_g])
    kb.act(gg[:], gg[:], AF.Exp, [r_g], [r_g])
    kb.act(gg[:], gg[:], AF.Ln, [r_g], [r_g], bias=1.0, scale=1.0)
    kb.act(nal[:], alog[:], AF.Exp, [r_alog], [r_nal])
    kb.ts("dve", nal[:], nal[:], -1.0, None, ALU.mult, None, [r_nal], [r_nal])
    kb.tt("dve", gg[:], gg[:], nal[:].unsqueeze(1).to_broadcast([128, 16, 6]), ALU.mult, [r_g, r_nal], [r_g])
    s0, rs0 = banks[6]
    s4, rs4 = banks[7]
    for i in range(16):
        kb.mm(s0[:, i * 6:(i + 1) * 6], triu[:], gg[:, i, :], True, True, [r_triu, r_g], [rs0])
        kb.mm(s4[:, i * 6:(i + 1) * 6], ones[:], gg[:, i, :], True, True, [r_ones, r_g], [rs4])
    kb.cp("dve", gc[:].rearrange("p t h -> p (t h)"), s0[:, 0:96], [rs0], [r_gc, rs0])
    kb.cp("dve", gl[:].rearrange("p t h -> p (t h)"), s4[:, 0:96], [rs4], [r_gl, rs4])
    kb.act(bgs[:], gc[:], AF.Exp, [r_gc], [r_bg])
    kb.tt("dve", bgs[:], bgs[:], beta[:], ALU.mult, [r_bg, r_beta], [r_bg])
    kb.tt("dve", kds[:], gl[:], gc[:], ALU.subtract, [r_gl, r_gc], [r_kds])
    kb.act(kds[:], kds[:], AF.Exp, [r_kds], [r_kds])
    kb.act(egl[:], gl[:], AF.Exp, [r_gl], [r_egl])
    if stop == "prologue":
        dbg = P.sb([128, 16, 6 * 5], F32, "dbg")
        rd = Res()
        for j, (t_, r_) in enumerate([(gc, r_gc), (gl, r_gl), (bgs, r_bg), (kds, r_kds), (egl, r_egl)]):
            kb.cp("dve", dbg[:, :, j * 6:(j + 1) * 6], t_[:], [r_], [rd])
        kb.dma("sp", kb.XM[:, 0:30].rearrange("(t p) c -> p t c", p=128), dbg[:], [rd], (), rd)
        P.close()
        return
    names1 = ["TriG", "kbg", "vb", "t1", "t2", "EG", "Ds", "DT", "M", "MTs", "TT", "Pa", "PTa", "Pb", "PTb",
              "vnew", "y", "y2", "junk", "S"]
    names2 = ["kdec", "attnT", "qgT", "wT", "u"]
    T = []
    for h in range(6):
        d = {}
        for n in names1:
            d[n] = (P.sb([128, 128], F32, n), Res())
        for n in names2:
            d[n] = [(P.sb([128, 128], F32, n), Res()) for _ in range(2)]
        d["ss"] = (P.sb([128, 1], F32, "ss"), Res())
        d["rs"] = (P.sb([128, 1], F32, "rs"), Res())
        T.append(d)
        kb.memset("pool", d["S"][0][:], 0.0, [d["S"][1]])
    ctst = P.sb([128, 6, SEQ], BF16, "ctst")
    r_ct = [Res() for _ in range(6)]
    xtr = [(P.sb([128, 18, 128], F32, "xt"), Res()) for _ in range(2)]
    ztr = [(P.sb([128, 768], F32, "zt"), Res()) for _ in range(2)]

    def load_tile(i):
        xt, rxt = xtr[i % 2]
        kb.dma("sp", xt[:], kb.GQ[:, i * 128:(i + 1) * 128].rearrange("(c p) t -> p c t", p=128), (), [rxt], rxt)

    def load_z(i):
        zt, rzt = ztr[i % 2]
        kb.dma("sp", zt[:], kb.TM[i * 128:(i + 1) * 128, 0:768], (), [rzt], rzt)
        kb.act(zt[:], zt[:], AF.Silu, [rzt], [rzt])

    def par_gen(h, i):
        d = T[h]
        sl = slots[h]
        par = i % 2
        xt, rxt = xtr[par]
        qn, kn, vT = xt[:, h, :], xt[:, 6 + h, :], xt[:, 12 + h, :]
        sc = lambda t: t[:, i, h:h + 1]
        pk, rpk = sl.next()
        kb.tr(pk, kn, ident[:], [rxt, r_id], [rpk])
        pv, rpv = sl.next()
        kb.tr(pv, vT, ident[:], [rxt, r_id], [rpv])
        TriG, rTriG = d["TriG"]
        kb.ts("dve", TriG[:], triu[:], sc(gg), None, ALU.mult, None, [r_triu, r_g], [rTriG])
        yield
        pG, rpG = sl.next()
        kb.mm(pG, negones[:], TriG[:], True, True, [r_nones, rTriG], [rpG])
        pKK, rpKK = sl.next()
        kb.mm(pKK, kn, kn, True, True, [rxt], [rpKK])
        kdec, rkdec = d["kdec"][par]
        kb.ts("dve", kdec[:], pk, sc(kds), None, ALU.mult, None, [rpk, r_kds], [rkdec, rpk])
        vb, rvb = d["vb"]
        kb.ts("dve", vb[:], pv, sc(beta), None, ALU.mult, None, [rpv, r_beta], [rvb, rpv])
        kbg, rkbg = d["kbg"]
        kb.ts("dve", kbg[:], pk, sc(bgs), None, ALU.mult, None, [rpk, r_bg], [rkbg, rpk])
        yield
        t1, rt1 = d["t1"]
        t2, rt2 = d["t2"]
        EG, rEG = d["EG"]
        kb.stt("dve", t1[:], pG, sc(gc), msl[:], ALU.add, ALU.add, [rpG, r_gc, r_msl], [rt1, rpG])
        kb.stt("dve", t2[:], pG, sc(gc), mui[:], ALU.add, ALU.add, [rpG, r_gc, r_mui], [rt2, rpG])
        kb.act(EG[:], pG, AF.Exp, [rpG], [rEG, rpG], scale=-1.0)
        yield
        Ds, rDs = d["Ds"]
        DT, rDT = d["DT"]
        kb.act(Ds[:], t1[:], AF.Exp, [rt1], [rDs])
        kb.act(DT[:], t2[:], AF.Exp, [rt2], [rDT], scale=-1.0)
        pKQ, rpKQ = sl.next()
        kb.mm(pKQ, kn, qn, True, True, [rxt], [rpKQ])
        yield
        M, rM = d["M"]
        attnT, rattnT = d["attnT"][par]
        qgT, rqgT = d["qgT"][par]
        kb.stt("dve", M[:], pKK, sc(nbeta), Ds[:], ALU.mult, ALU.mult, [rpKK, r_nbeta, rDs], [rM, rpKK])
        kb.tt("dve", attnT[:], pKQ, DT[:], ALU.mult, [rpKQ, rDT], [rattnT, rpKQ])
        kb.tt("pool", qgT[:], qn, EG[:], ALU.mult, [rxt, rEG], [rqgT])
        yield
        pMT, rpMT = sl.next()
        kb.tr(pMT, M[:], ident[:], [rM, r_id], [rpMT])
        yield
        MTs, rMTs = d["MTs"]
        TT, rTT = d["TT"]
        kb.cp("act", MTs[:], pMT, [rpMT], [rMTs, rpMT])
        kb.tt("dve", TT[:], pMT, ident[:], ALU.add, [rpMT, r_id], [rTT, rpMT])
        yield
        Pc, rPc = M, rM
        PTc, rPTc = MTs, rMTs
        bufs = [(d["Pa"], d["PTa"]), (d["Pb"], d["PTb"])]
        for lev in range(1, 7):
            (Pn, rPn), (PTn, rPTn) = bufs[lev % 2]
            pP, rpP = sl.next()
            kb.mm(pP, PTc[:], Pc[:], True, True, [rPTc, rPc], [rpP])
            if lev < 6:
                pPT, rpPT = sl.next()
                kb.mm(pPT, Pc[:], PTc[:], True, True, [rPTc, rPc], [rpPT])
            yield
            kb.cp("act", Pn[:], pP, [rpP], [rPn, rpP])
            if lev < 6:
                kb.cp("dve", PTn[:], pPT, [rpPT], [rPTn, rpPT])
            yield
            pU, rpU = sl.next()
            kb.mm(pU, Pn[:], TT[:], True, True, [rPn, rTT], [rpU])
            yield
            kb.tt("dve", TT[:], TT[:], pU, ALU.add, [rTT, rpU], [rTT, rpU])
            yield
            Pc, rPc, PTc, rPTc = Pn, rPn, PTn, rPTn
        pW, rpW = sl.next()
        kb.mm(pW, kbg[:], TT[:], True, True, [rkbg, rTT], [rpW])
        pu, rpu = sl.next()
        kb.mm(pu, TT[:], vb[:], True, True, [rTT, rvb], [rpu])
        yield
        wT, rwT = d["wT"][par]
        u, ru = d["u"][par]
        kb.cp("act", wT[:], pW, [rpW], [rwT, rpW])
        kb.cp("dve", u[:], pu, [rpu], [ru, rpu])
        yield

    def seq_gen(h, i):
        d = T[h]
        sl = slots[h]
        par = i % 2
        sc = lambda t: t[:, i, h:h + 1]
        Ss, rS = d["S"]
        wT, rwT = d["wT"][par]
        u, ru = d["u"][par]
        qgT, rqgT = d["qgT"][par]
        attnT, rattnT = d["attnT"][par]
        kdec, rkdec = d["kdec"][par]
        vnew, rvn = d["vnew"]
        y, ry = d["y"]
        y2, ry2 = d["y2"]
        junk, rj = d["junk"]
        ss, rss = d["ss"]
        rs, rrs = d["rs"]
        zt, rzt = ztr[par]
        pwS, rpwS = sl.next()
        kb.mm(pwS, wT[:], Ss[:], True, True, [rwT, rS], [rpwS])
        kb.memset("pool", ss[:], 0.0, [rss])
        yield
        kb.tt("dve", vnew[:], u[:], pwS, ALU.subtract, [ru, rpwS], [rvn, rpwS])
        yield
        po, rpo = sl.next()
        kb.mm(po, qgT[:], Ss[:], True, False, [rqgT, rS], [rpo])
        kb.mm(po, attnT[:], vnew[:], False, True, [rattnT, rvn], [rpo])
        pkv, rpkv = sl.next()
        kb.mm(pkv, kdec[:], vnew[:], True, True, [rkdec, rvn], [rpkv])
        yield
        kb.stt("dve", Ss[:], Ss[:], sc(egl), pkv, ALU.mult, ALU.add, [rS, r_egl, rpkv], [rS, rpkv])
        kb.act(junk[:], po, AF.Square, [rpo, rss], [rj, rss, rpo], accum=ss[:])
        yield
        kb.act(rs[:], ss[:], AF.Sqrt, [rss], [rrs], bias=EPS, scale=1.0 / 128)
        yield
        kb.recip(rs[:], rs[:], [rrs], [rrs])
        yield
        kb.stt("dve", y[:], po, rs[:, 0:1], grow[:], ALU.mult, ALU.mult, [rpo, rrs, r_grow], [ry, rpo])
        yield
        kb.tt("pool", y2[:], y[:], zt[:, h * 128:(h + 1) * 128], ALU.mult, [ry, rzt], [ry2])
        yield
        pyT, rpyT = sl.next()
        kb.tr(pyT, y2[:], ident[:], [ry2, r_id], [rpyT])
        yield
        kb.cp("act", ctst[:, h, i * 128:(i + 1) * 128], pyT, [rpyT], [r_ct[h], rpyT])
        yield

    load_tile(0)
    _lockstep([par_gen(h, 0) for h in range(6)], maxrounds)
    if stop == "par0":
        for h in range(6):
            for j, nm in enumerate(["wT", "u", "attnT", "qgT", "kdec"]):
                t_, r_ = T[h][nm][0]
                kb.cp("act", ctst[:, h, j * 128:(j + 1) * 128], t_[:], [r_], [r_ct[h]])
            kb.dma("sp", kb.CT[h * 128:(h + 1) * 128, :], ctst[:, h, :], [r_ct[h]], (), r_ct[h])
        P.close()
        return
    for i in range(16):
        load_z(i)
        gens = [seq_gen(h, i) for h in range(6)]
        if i + 1 < 16:
            load_tile(i + 1)
            gens = gens + [par_gen(h, i + 1) for h in range(6)]
        _lockstep(gens)
    for h in range(6):
        kb.dma("sp", kb.CT[h * 128:(h + 1) * 128, :], ctst[:, h, :], [r_ct[h]], (), r_ct[h])
    P.close()


def phase_tables(kb, layers=()):
    P = Phase(kb)
    cpool = Ring([Res() for _ in range(34)])
    for l_ in layers:
        for w_ in ("o", "g", "u", "d"):
            cast_weights(kb, l_, w_, cpool)
    rb, r_rb = P.load_const(kb.relb_rep, [128, 192])
    E = P.sb([128, 32, 6], F32, "E")
    r_E = Res()
    kb.tt("dve", E[:], rb[:].rearrange("p (b h) -> p b h", h=6),
          rb[:, 186:192].unsqueeze(1).to_broadcast([128, 32, 6]), ALU.subtract, [r_rb], [r_E])
    kb.act(E[:], E[:], AF.Exp, [r_E], [r_E])
    idxC, r_ic = P.load_const(kb.c_idxC, [127, 2048])
    idxW, r_iw = P.load_const(kb.c_idxW, [128, 2, 128])
    TC = P.sb([127, 6, 2048], F32, "TC")
    r_TC = [Res() for _ in range(6)]
    TW = P.sb([128, 2, 6, 128], F32, "TW")
    r_TW = [Res() for _ in range(6)]
    for h in range(6):
        kb.memset("dve", TC[:, h, :], 0.0, [r_TC[h]])
        kb.memset("dve", TW[:, :, h, :], 0.0, [r_TW[h]])
    mk = Ring([(P.sb([127, 2048], F32, "mk"), Res()) for _ in range(2)])
    mw = Ring([(P.sb([128, 2, 128], F32, "mw"), Res()) for _ in range(2)])
    for b in range(32):
        m, rm = mk.next()
        kb.ts("dve", m[:], idxC[:], float(b), None, ALU.is_equal, None, [r_ic], [rm])
        m2, rm2 = mw.next()
        kb.ts("dve", m2[:], idxW[:], float(b), None, ALU.is_equal, None, [r_iw], [rm2])
        for h in range(6):
            kb.stt("dve", TC[:, h, :], m[:], E[0:127, b, h:h + 1], TC[:, h, :], ALU.mult, ALU.add, [rm, r_E, r_TC[h]], [r_TC[h]])
            kb.stt("dve", TW[:, :, h, :], m2[:], E[:, b, h:h + 1], TW[:, :, h, :], ALU.mult, ALU.add, [rm2, r_E, r_TW[h]], [r_TW[h]])
    TCb = P.sb([127, 6, 2048], BF16, "TCb")
    TWb = P.sb([128, 2, 6, 128], BF16, "TWb")
    r_b = Res()
    r_b2 = Res()
    for h in range(6):
        kb.cp("act", TCb[:, h, :], TC[:, h, :], [r_TC[h]], [r_b])
        kb.cp("act", TWb[:, :, h, :], TW[:, :, h, :], [r_TW[h]], [r_b2])
    kb.dma("sp", kb.TCd, TCb[:].rearrange("p h t -> p (h t)"), [r_b], (), r_b)
    kb.dma("sp", kb.TWd, TWb[:].rearrange("p c h t -> p (c h t)"), [r_b2], (), r_b2)
    P.close()


def phase_nsa(kb, l, hk, casts=()):
    P = Phase(kb)
    for w_ in casts:
        cast_weights(kb, l, w_)
    SC = 128.0 ** -0.5
    ident, r_id = P.load_const(kb.c_ident, [128, 128])
    ones = P.sb([128, 128], F32, "ones")
    r_ones = Res()
    kb.memset("pool", ones[:], 1.0, [r_ones])
    qn, r_qn = P.load_const(kb.qnormT[l], [128, 1])
    kn, r_kn = P.load_const(kb.knormT[l], [128, 3])
    k0row, r_k0 = P.load_const(kb.knorm0_row[l], [128, 128])
    keep, r_keep = P.load_const(kb.c_keep, [128, 16, 32])
    addt, r_add = P.load_const(kb.c_add, [128, 16, 32])
    E32, r_E32 = P.load_const(kb.c_E32, [32, 2048], BF16, q="pool")
    mW0, r_mW0 = P.load_const(kb.c_maskW0, [128, 128], BF16, q="pool")
    TCb, r_TCb = P.load_const(kb.TCd[:, hk * 3 * 2048:(hk * 3 + 3) * 2048].rearrange("p (h t) -> p h t", h=3), [127, 3, 2048], BF16)
    TWb, r_TWb = P.load_const(kb.TWd.rearrange("p (c h t) -> p c h t", c=2, h=6)[:, :, hk * 3:hk * 3 + 3, :], [128, 2, 3, 128], BF16)
    gt, r_gt = P.load_const(kb.TM[:, TM_GATE:TM_GATE + 18].rearrange("(c p) g -> p c g", p=128), [128, 16, 18])
    kb.act(gt[:], gt[:], AF.Sigmoid, [r_gt], [r_gt])
    banks = P.banks(8)
    qT3 = P.sb([128, 3, SEQ], BF16, "qT3")
    r_q3 = Res()
    kslcT = P.sb([128, SEQ], BF16, "kslcT")
    kwinT = P.sb([128, SEQ], BF16, "kwinT")
    r_ks, r_kw = Res(), Res()
    raw = Ring([(P.sb([128, SEQ], F32, "raw"), Res()) for _ in range(2)])
    sqr = Ring([(P.sb([128, SEQ], F32, "sq"), Res()) for _ in range(2)])
    rnr = Ring([(P.sb([128, SEQ], F32, "rn"), Res()) for _ in range(2)])
    bgrp = Ring([banks[0:4], banks[4:8]])

    def normed(row0, gain_ap, r_gain, out_ap, r_out):
        x, rx = raw.next()
        kb.dma("sp", x[:], kb.FT[row0:row0 + 128, :], (), [rx], rx)
        sq, rs = sqr.next()
        rn, rr = rnr.next()
        kb.tt("pool", sq[:], x[:], x[:], ALU.mult, [rx], [rs])
        bg = bgrp.next()
        for n in range(4):
            kb.mm(bg[n][0][:], ones[:], sq[:, n * 512:(n + 1) * 512], True, True, [r_ones, rs], [bg[n][1]])
        for n in range(4):
            kb.act(rn[:, n * 512:(n + 1) * 512], bg[n][0][:], AF.Sqrt, [bg[n][1]], [rr, bg[n][1]], bias=EPS, scale=1.0 / 128)
        kb.recip(rn[:], rn[:], [rr], [rr])
        kb.stt("dve", out_ap, x[:], gain_ap, rn[:], ALU.mult, ALU.mult, [rx, r_gain, rr, r_out], [r_out])

    for g in range(3):
        normed(R_NQ + (hk * 3 + g) * 128, qn[:, 0:1], r_qn, qT3[:, g, :], r_q3)
    normed(R_KSLC + hk * 128, kn[:, 1:2], r_kn, kslcT[:], r_ks)
    normed(R_KWIN + hk * 128, kn[:, 2:3], r_kn, kwinT[:], r_kw)
    vs = P.sb([128, 16, 129], BF16, "vs")
    vw = P.sb([128, 16, 129], BF16, "vw")
    r_vs, r_vw = Res(), Res()
    kb.dma("pool", vs[:, :, 0:128], kb.TM[:, TM_VSLC + hk * 128:TM_VSLC + (hk + 1) * 128].rearrange("(c p) d -> p c d", p=128),
           (), [r_vs], r_vs)
    kb.dma("pool", vw[:, :, 0:128], kb.TM[:, TM_VWIN + hk * 128:TM_VWIN + (hk + 1) * 128].rearrange("(c p) d -> p c d", p=128),
           (), [r_vw], r_vw)
    kb.memset("pool", vs[:, :, 128:129], 1.0, [r_vs])
    kb.memset("pool", vw[:, :, 128:129], 1.0, [r_vw])
    kcT = P.sb([128, 127], BF16, "kcT")
    r_kcT = Res()
    vc = P.sb([128, 161], BF16, "vc")
    r_vc = Res()
    kb.dma("pool", vc[0:127, 129:161], kb.c_ovl, (), [r_vc], r_vc)
    kb.memset("pool", vc[:, 128:129], 1.0, [r_vc])
    tokA = P.sb([128, SEQ], BF16, "tokA")
    tokB = P.sb([128, SEQ], BF16, "tokB")
    hidT = P.sb([128, 128], BF16, "hidT")
    kcn = P.sb([128, 128], F32, "kcn")
    junk = P.sb([128, 128], F32, "junk")
    ss1 = P.sb([128, 1], F32, "ss1")
    r_tA, r_tB, r_hid, r_kcn, r_junk, r_ss1 = Res(), Res(), Res(), Res(), Res(), Res()
    for kv in range(2):
        x, rx = raw.next()
        row0 = (R_KCMP if kv == 0 else R_VCMP) + hk * 128
        kb.dma("sp", x[:], kb.FT[row0:row0 + 128, :], (), [rx], rx)
        pe, r_pe = P.load_const(kb.cposT[l, kv], [128, 32])
        w1, r_w1 = P.load_const(kb.cmp_w1[l, kv].rearrange("(j d) e -> d j e", d=128), [128, 32, 128], BF16, q="pool")
        w2, r_w2 = P.load_const(kb.cmp_w2[l, kv], [128, 128], BF16, q="pool")
        xv = x[:].rearrange("p (n j) -> p n j", j=16)
        kb.tt("dve", tokA[:].rearrange("p (n j) -> p n j", j=16), xv, pe[:, 0:16].unsqueeze(1).to_broadcast([128, 128, 16]),
              ALU.add, [rx, r_pe], [r_tA])
        kb.tt("dve", tokB[:].rearrange("p (n j) -> p n j", j=16), xv, pe[:, 16:32].unsqueeze(1).to_broadcast([128, 128, 16]),
              ALU.add, [rx, r_pe], [r_tB])
        bk, rb = banks[0]
        for j in range(32):
            src = tokA if j < 16 else tokB
            kb.mm(bk[:, 0:127], w1[:, j, :], src[:, j:j + 16 * 126 + 1:16], j == 0, j == 31, [r_w1, r_tA, r_tB], [rb])
        kb.act(hidT[:, 0:127], bk[:, 0:127], AF.Silu, [rb], [r_hid, rb])
        bk2, rb2 = banks[1]
        kb.mm(bk2[0:127, 0:128], hidT[:, 0:127], w2[:], True, True, [r_hid, r_w2], [rb2])
        if kv == 0:
            kb.memset("pool", ss1[:], 0.0, [r_ss1])
            kb.act(junk[0:127, :], bk2[0:127, 0:128], AF.Square, [rb2, r_ss1], [r_junk, r_ss1, rb2], accum=ss1[0:127, :])
            kb.act(ss1[0:127, :], ss1[0:127, :], AF.Sqrt, [r_ss1], [r_ss1], bias=EPS, scale=1.0 / 128)
            kb.recip(ss1[0:127, :], ss1[0:127, :], [r_ss1], [r_ss1])
            kb.stt("dve", kcn[0:127, :], bk2[0:127, 0:128], ss1[0:127, 0:1], k0row[0:127, :], ALU.mult, ALU.mult,
                   [rb2, r_ss1, r_k0], [r_kcn, rb2])
            bk3, rb3 = banks[2]
            kb.tr(bk3[:, 0:127], kcn[0:127, :], ident[0:127, 0:127], [r_kcn, r_id], [rb3])
            kb.cp("act", kcT[:], bk3[:, 0:127], [rb3], [r_kcT, rb3])
        else:
            kb.cp("act", vc[0:127, 0:128], bk2[0:127, 0:128], [rb2], [r_vc, rb2])
    pTr = Ring([(P.sb([128, 384], BF16, "pT"), Res()) for _ in range(4)])
    scr = Ring(banks[0:3])
    accC, r_accC = banks[3]
    accW, r_accW = banks[4]
    accS, r_accS = banks[5]
    m1, r_m1 = banks[6]
    m2, r_m2 = banks[7]
    nmT = P.sb([32, 3, 128], BF16, "nmT")
    r_nmT = Res()
    oacc = P.sb([128, 3, 128], F32, "oacc")
    r_oacc = Res()
    ctst = P.sb([128, 3, SEQ], BF16, "ctst")
    r_ct = Res()
    imp = P.sb([128, 32], F32, "imp")
    top8 = P.sb([128, 8], F32, "top8")
    negm = P.sb([128, 32], F32, "negm")
    rden = P.sb([128, 3], F32, "rden")
    fac = P.sb([128, 3], F32, "fac")
    r_imp, r_top8, r_negm, r_rden, r_fac = Res(), Res(), Res(), Res(), Res()

    def q_rhs(i):
        return qT3[:, :, i * 128:(i + 1) * 128]

    jobs = []
    for i in range(16):
        jobs.append(("cmp", i, 0, True, True))
        wj = [c for c in range(5) if i - 4 + c >= 0]
        for c in wj:
            jobs.append(("win", i, c, c == wj[0], c == wj[-1]))
        for kc in range(i + 1):
            jobs.append(("slc", i, kc, kc == 0, kc == i))

    def stage_a(job):
        kind, i, c, first, last = job
        bk, rb = scr.next()
        if kind == "cmp":
            kb.mm(bk[0:127, 0:384].rearrange("p (g t) -> p g t", g=3), kcT[:], q_rhs(i), True, True, [r_kcT, r_q3], [rb])
        elif kind == "win":
            kc = i - 4 + c
            kb.mm(bk[:, 0:384].rearrange("p (g t) -> p g t", g=3), kwinT[:, kc * 128:(kc + 1) * 128], q_rhs(i), True, True, [r_kw, r_q3], [rb])
        else:
            o3 = bk[:, 0:384].rearrange("p (g t) -> p g t", g=3)
            kb.mm(o3, kslcT[:, c * 128:(c + 1) * 128], q_rhs(i), True, False, [r_ks, r_q3], [rb])
            kb.mm(o3, E32[:, c * 128:(c + 1) * 128], nmT[:], False, True, [r_E32, r_nmT], [rb])
        return bk, rb

    def stage_b(job, bk, rb):
        kind, i, c, first, last = job
        pT, rp = pTr.next()
        np_ = 127 if kind == "cmp" else 128
        kb.act(pT[0:np_, :], bk[0:np_, 0:384], AF.Exp, [rb], [rp, rb], scale=SC)
        p3 = pT[0:np_, :].rearrange("p (g t) -> p g t", g=3)
        if kind == "cmp":
            kb.tt("dve", p3, p3, TCb[:, :, i * 128:(i + 1) * 128], ALU.mult, [rp, r_TCb], [rp])
        elif kind == "win":
            if c == 0:
                kb.tt("dve", p3, p3, mW0[:].unsqueeze(1).to_broadcast([128, 3, 128]), ALU.mult, [rp, r_mW0], [rp])
            elif c >= 3:
                kb.tt("dve", p3, p3, TWb[:, c - 3, :, :], ALU.mult, [rp, r_TWb], [rp])
        else:
            if c == i:
                kb.tt("dve", p3, p3, TWb[:, 1, :, :], ALU.mult, [rp, r_TWb], [rp])
            elif c == i - 1:
                kb.tt("dve", p3, p3, TWb[:, 0, :, :], ALU.mult, [rp, r_TWb], [rp])
        return pT, rp

    def pv(out, lhsT, rhs, start, reads, writes):
        return kb.S.op("pe", lambda e: e.matmul(out, lhsT=lhsT, rhs=rhs, start=start, stop=True, skip_group_check=True),
                       reads, writes)

    def combine(acc, r_acc, br, i, firstbr):
        w = 161 if br == 0 else 129
        kb.ts("dve", rden[:], acc[:, 128:128 + 2 * w + 1:w], 1e-30, None, ALU.max, None, [r_acc], [r_rden, r_acc])
        kb.recip(rden[:], rden[:], [r_rden], [r_rden])
        kb.tt("dve", fac[:], rden[:], gt[:, i, br * 6 + hk * 3:br * 6 + hk * 3 + 3], ALU.mult, [r_rden, r_gt], [r_fac])
        for g in range(3):
            if firstbr:
                kb.ts("dve", oacc[:, g, :], acc[:, g * w:g * w + 128], fac[:, g:g + 1], None, ALU.mult, None,
                      [r_acc, r_fac], [r_oacc, r_acc])
            else:
                kb.stt("dve", oacc[:, g, :], acc[:, g * w:g * w + 128], fac[:, g:g + 1], oacc[:, g, :], ALU.mult, ALU.add,
                       [r_acc, r_fac, r_oacc], [r_oacc, r_acc])

    def stage_c(job, pT, rp):
        kind, i, c, first, last = job
        if kind == "cmp":
            for g in range(3):
                pv(accC[:, g * 161:(g + 1) * 161], pT[0:127, g * 128:(g + 1) * 128], vc[0:127, :], True, [rp, r_vc], [r_accC])
            combine(accC, r_accC, 0, i, True)
            for g in range(3):
                if g == 0:
                    kb.ts("dve", imp[:], accC[:, 129:161], rden[:, 0:1], None, ALU.mult, None, [r_accC, r_rden], [r_imp, r_accC])
                else:
                    kb.stt("dve", imp[:], accC[:, g * 161 + 129:g * 161 + 161], rden[:, g:g + 1], imp[:], ALU.mult, ALU.add,
                           [r_accC, r_rden, r_imp], [r_imp, r_accC])
            kb.tt("dve", imp[:], imp[:], keep[:, i, :], ALU.mult, [r_imp, r_keep], [r_imp])
            kb.tt("dve", imp[:], imp[:], addt[:, i, :], ALU.add, [r_imp, r_add], [r_imp])
            kb.S.op("dve", lambda e: e.max(out=top8[:], in_=imp[:]), [r_imp], [r_top8])
            kb.ts("dve", negm[:], imp[:], top8[:, 7:8], -30000.0, ALU.is_lt, ALU.mult, [r_imp, r_top8], [r_negm])
            kb.tr(m1[0:32, 0:128], negm[:], ident[:], [r_negm, r_id], [r_m1])
            kb.cp("act", nmT[:], m1[0:32, 0:128].unsqueeze(1).to_broadcast([32, 3, 128]), [r_m1], [r_nmT, r_m1])
            return
        acc, r_acc, vv, r_vv = (accW, r_accW, vw, r_vw) if kind == "win" else (accS, r_accS, vs, r_vs)
        kc = (i - 4 + c) if kind == "win" else c
        for g in range(3):
            pv(acc[:, g * 129:(g + 1) * 129], pT[:, g * 128:(g + 1) * 128], vv[:, kc, :], first and g == 0, [rp, r_vv], [r_acc])
        if last:
            combine(acc, r_acc, 2 if kind == "win" else 1, i, False)
            if kind == "slc":
                for g in range(3):
                    kb.tr(m2[:, g * 128:(g + 1) * 128], oacc[:, g, :], ident[:], [r_oacc, r_id], [r_m2])
                kb.cp("act", ctst[:, :, i * 128:(i + 1) * 128], m2[:, 0:384].rearrange("p (g t) -> p g t", g=3), [r_m2], [r_ct, r_m2])

    LA = 2
    queue = []
    ja = 0
    cmp_done = [False] * 16
    for j, job in enumerate(jobs):
        while ja < len(jobs) and len(queue) < LA + 1:
            nj = jobs[ja]
            if nj[0] == "slc" and nj[2] == 0 and not cmp_done[nj[1]]:
                break
            queue.append(stage_a(nj))
            ja += 1
        cur = queue.pop(0)
        pT, rp = stage_b(job, *cur)
        stage_c(job, pT, rp)
        if job[0] == "cmp":
            cmp_done[job[1]] = True
    for g in range(3):
        h = hk * 3 + g
        kb.dma("sp", kb.CT[768 + h * 128:768 + (h + 1) * 128, :], ctst[:, g, :], [r_ct], (), r_ct)
    P.close()


def build_all(nseq=2, layers=(0, 1, 2, 3), debug=False):
    kb = KB(nseq=nseq, debug=debug)
    phase_tables(kb, layers)
    for li, l in enumerate(layers):
        first = li == 0
        last = li == len(layers) - 1
        for s in range(nseq):
            phase_ab(kb, l, s, first)
            phase_sconv(kb, l)
            phase_g1(kb, l)
            phase_g2(kb, l)
            phase_nsa(kb, l, 0)
            phase_nsa(kb, l, 1)
            for nb in range(4):
                phase_cd(kb, l, s, nb, first, last)
    return kb


_CACHE = {}


def kernel(**inputs):
    x = np.asarray(inputs["x"], dtype=np.float32)
    B = x.shape[0]
    nseq = B // NCORES
    sh = shared_inputs(inputs)
    if "kb" not in _CACHE:
        _CACHE["kb"] = build_all(nseq=nseq)
    kb = _CACHE["kb"]
    in_maps = []
    for c in range(NCORES):
        m = dict(sh)
        m["xT"] = np.ascontiguousarray(x[c * nseq:(c + 1) * nseq].reshape(nseq * SEQ, D).T)
        in_maps.append(m)
    res = run_bass_kernel_spmd(kb.nc, in_maps, core_ids=list(range(NCORES)))
    out = np.empty((B, SEQ, D), np.float32)
    for c in range(NCORES):
        out[c * nseq:(c + 1) * nseq] = np.asarray(res.results[c]["outT"]).T.reshape(nseq, SEQ, D)
    return out
```
